# Optimizing a Trainium2 kernel written in Bass

```python
import math
import jax, jax.numpy as jnp
from jax import lax
import numpy as np

D_MODEL = 1024
BATCH = 8
SEQ = 2048
DEPTH = 2

CHUNK = 64
N_META = 16
Q_BLOCK = 128
EPS = 1e-5

N_A_LAYERS = DEPTH // 2
N_B_LAYERS = DEPTH - N_A_LAYERS

DIFF_HEAD_DIM = 64
DIFF_HEADS = D_MODEL // (2 * DIFF_HEAD_DIM)
DIFF_V_DIM = 2 * DIFF_HEAD_DIM
ROPE_THETA = 500000.0
ROT_DIM = DIFF_HEAD_DIM // 4

MLA_HEADS = D_MODEL // 64
MLA_NOPE = 64
MLA_ROPE = 32
MLA_V = 64
Q_LORA = 3 * D_MODEL // 8
KV_LORA = D_MODEL // 4
MLA_ROPE_THETA = 10000.0

D_FF = 128 * ((8 * D_MODEL // 3 + 127) // 128)
CONV_W = 3

kernel_name = "yoco_diffattn_mla_convffn_meta"


def rms_norm(x, g):
    xf = x.astype(jnp.float32)
    y = xf * lax.rsqrt(jnp.mean(xf * xf, axis=-1, keepdims=True) + EPS)
    return (y * g.astype(jnp.float32)).astype(x.dtype)


def rotary(x, pos, theta, rot_dim):
    half = rot_dim // 2
    inv = 1.0 / (theta ** (jnp.arange(half, dtype=jnp.float32) / half))
    ang = pos.astype(jnp.float32)[:, None] * inv[None, :]
    cos = jnp.cos(ang)[None, :, None, :]
    sin = jnp.sin(ang)[None, :, None, :]
    xr = x[..., :rot_dim].astype(jnp.float32)
    x1, x2 = xr[..., :half], xr[..., half:]
    rot = jnp.concatenate([x1 * cos - x2 * sin, x2 * cos + x1 * sin], axis=-1).astype(x.dtype)
    return jnp.concatenate([rot, x[..., rot_dim:]], axis=-1)


def chunk_ids(pos):
    return jnp.where(pos < N_META, 0, 1 + (pos - N_META) // CHUNK)


def to_query_blocks(q, n_blocks):
    B, L, H, d = q.shape
    q = jnp.pad(q, ((0, 0), (0, n_blocks * Q_BLOCK - L), (0, 0), (0, 0)))
    return q.reshape(B, n_blocks, Q_BLOCK, H, d).transpose(1, 0, 2, 3, 4)


def from_query_blocks(o, L):
    nb, B, qb, H, d = o.shape
    return o.transpose(1, 0, 2, 3, 4).reshape(B, nb * qb, H, d)[:, :L]


def diff_attention(q1, q2, k1, k2, v, lam, q_chunk_blocks, k_chunk):
    L = k1.shape[1]
    nb = q_chunk_blocks.shape[0]
    scale = DIFF_HEAD_DIM ** -0.5

    def block(args):
        qb1, qb2, qc = args
        mask = k_chunk[None, :] <= qc[:, None]

        def probs(qb, k):
            s = jnp.einsum('bqhd,bkhd->bhqk', qb, k).astype(jnp.float32) * scale
            return jax.nn.softmax(jnp.where(mask, s, -jnp.inf), axis=-1)

        a = probs(qb1, k1) - lam * probs(qb2, k2)
        return jnp.einsum('bhqk,bkhd->bqhd', a.astype(v.dtype), v)

    o = lax.map(block, (to_query_blocks(q1, nb), to_query_blocks(q2, nb), q_chunk_blocks))
    return from_query_blocks(o, L)


def mla_attention(q_nope, q_rope, k_nope, k_rope, v, q_chunk_blocks, k_chunk):
    L = k_nope.shape[1]
    nb = q_chunk_blocks.shape[0]
    scale = (MLA_NOPE + MLA_ROPE) ** -0.5

    def block(args):
        qn, qr, qc = args
        mask = k_chunk[None, :] <= qc[:, None]
        s = (jnp.einsum('bqhd,bkhd->bhqk', qn, k_nope)
             + jnp.einsum('bqhr,bkr->bhqk', qr, k_rope)).astype(jnp.float32) * scale
        p = jax.nn.softmax(jnp.where(mask, s, -jnp.inf), axis=-1)
        return jnp.einsum('bhqk,bkhd->bqhd', p.astype(v.dtype), v)

    o = lax.map(block, (to_query_blocks(q_nope, nb), to_query_blocks(q_rope, nb), q_chunk_blocks))
    return from_query_blocks(o, L)


def diff_attn_layer(h, g_norm, w_qkv, lq1, lk1, lq2, lk2, g_sub, w_o, layer_idx, pos, qcb, kc):
    B, L, _ = h.shape
    H, d = DIFF_HEADS, DIFF_HEAD_DIM
    xn = rms_norm(h, g_norm)
    qkv = xn @ w_qkv
    q, k, v = jnp.split(qkv, [2 * H * d, 4 * H * d], axis=-1)
    q = rotary(q.reshape(B, L, 2 * H, d), pos, ROPE_THETA, ROT_DIM).reshape(B, L, H, 2, d)
    k = rotary(k.reshape(B, L, 2 * H, d), pos, ROPE_THETA, ROT_DIM).reshape(B, L, H, 2, d)
    v = v.reshape(B, L, H, DIFF_V_DIM)
    lam_init = 0.8 - 0.6 * math.exp(-0.3 * layer_idx)
    lam = (jnp.exp(jnp.sum(lq1.astype(jnp.float32) * lk1.astype(jnp.float32)))
           - jnp.exp(jnp.sum(lq2.astype(jnp.float32) * lk2.astype(jnp.float32))) + lam_init)
    o = diff_attention(q[..., 0, :], q[..., 1, :], k[..., 0, :], k[..., 1, :], v, lam, qcb, kc)
    o = rms_norm(o, g_sub) * (1.0 - lam_init)
    return h + o.reshape(B, L, H * DIFF_V_DIM) @ w_o


def shared_latent_kv(h, g_norm, w_dkv, g_kv, w_ukv, pos):
    B, L, _ = h.shape
    xn = rms_norm(h, g_norm)
    ckv = xn @ w_dkv
    c_kv = rms_norm(ckv[..., :KV_LORA], g_kv)
    k_rope = rotary(ckv[..., None, KV_LORA:], pos, MLA_ROPE_THETA, MLA_ROPE)[:, :, 0]
    kv = (c_kv @ w_ukv).reshape(B, L, MLA_HEADS, MLA_NOPE + MLA_V)
    return kv[..., :MLA_NOPE], k_rope, kv[..., MLA_NOPE:]


def mla_layer(h, g_norm, w_dq, g_q, w_uq, w_o, k_nope, k_rope, v, pos, qcb, kc):
    B, L, _ = h.shape
    xn = rms_norm(h, g_norm)
    c_q = rms_norm(xn @ w_dq, g_q)
    q = (c_q @ w_uq).reshape(B, L, MLA_HEADS, MLA_NOPE + MLA_ROPE)
    q_nope = q[..., :MLA_NOPE]
    q_rope = rotary(q[..., MLA_NOPE:], pos, MLA_ROPE_THETA, MLA_ROPE)
    o = mla_attention(q_nope, q_rope, k_nope, k_rope, v, qcb, kc)
    return h + o.reshape(B, L, MLA_HEADS * MLA_V) @ w_o


def conv_ffn(h, g_norm, w_in, conv_w, conv_b, w_out):
    L = h.shape[1]
    xn = rms_norm(h, g_norm)
    a, b = jnp.split(xn @ w_in, 2, axis=-1)
    ap = jnp.pad(a, ((0, 0), (CONV_W - 1, 0), (0, 0)))
    a_conv = conv_b
    for j in range(CONV_W):
        a_conv = a_conv + ap[:, j:j + L] * conv_w[j]
    return h + (jax.nn.silu(a_conv) * b) @ w_out


def setup_inputs(seed: int = 0) -> dict:
    key = jax.random.key(seed)
    ks = iter(jax.random.split(key, 32))
    f32 = jnp.float32

    def nrm(shape, scale):
        return jax.random.normal(next(ks), shape, f32) * scale

    def gain(shape):
        return 1.0 + 0.02 * jax.random.normal(next(ks), shape, f32)

    nA, nB = N_A_LAYERS, N_B_LAYERS
    qkv_out = 4 * DIFF_HEADS * DIFF_HEAD_DIM + DIFF_HEADS * DIFF_V_DIM
    return {
        "x": nrm((BATCH, SEQ, D_MODEL), 1.0),
        "meta_tokens": nrm((N_META, D_MODEL), 1.0),
        "a_attn_norm": gain((nA, D_MODEL)),
        "a_w_qkv": nrm((nA, D_MODEL, qkv_out), D_MODEL ** -0.5),
        "a_lambda_q1": nrm((nA, DIFF_HEAD_DIM), 0.1),
        "a_lambda_k1": nrm((nA, DIFF_HEAD_DIM), 0.1),
        "a_lambda_q2": nrm((nA, DIFF_HEAD_DIM), 0.1),
        "a_lambda_k2": nrm((nA, DIFF_HEAD_DIM), 0.1),
        "a_sub_norm": gain((nA, DIFF_V_DIM)),
        "a_w_o": nrm((nA, DIFF_HEADS * DIFF_V_DIM, D_MODEL), (DIFF_HEADS * DIFF_V_DIM) ** -0.5),
        "kv_norm": gain((D_MODEL,)),
        "kv_w_dkv": nrm((D_MODEL, KV_LORA + MLA_ROPE), D_MODEL ** -0.5),
        "kv_a_norm": gain((KV_LORA,)),
        "kv_w_ukv": nrm((KV_LORA, MLA_HEADS * (MLA_NOPE + MLA_V)), KV_LORA ** -0.5),
        "b_attn_norm": gain((nB, D_MODEL)),
        "b_w_dq": nrm((nB, D_MODEL, Q_LORA), D_MODEL ** -0.5),
        "b_q_a_norm": gain((nB, Q_LORA)),
        "b_w_uq": nrm((nB, Q_LORA, MLA_HEADS * (MLA_NOPE + MLA_ROPE)), Q_LORA ** -0.5),
        "b_w_o": nrm((nB, MLA_HEADS * MLA_V, D_MODEL), (MLA_HEADS * MLA_V) ** -0.5),
        "ffn_norm": gain((DEPTH, D_MODEL)),
        "ffn_w_in": nrm((DEPTH, D_MODEL, 2 * D_FF), D_MODEL ** -0.5),
        "ffn_conv_w": nrm((DEPTH, CONV_W, D_FF), CONV_W ** -0.5),
        "ffn_conv_b": nrm((DEPTH, D_FF), 0.01),
        "ffn_w_out": nrm((DEPTH, D_FF, D_MODEL), D_FF ** -0.5),
        "final_norm": gain((D_MODEL,)),
    }


def reference(x, meta_tokens, a_attn_norm, a_w_qkv, a_lambda_q1, a_lambda_k1, a_lambda_q2,
              a_lambda_k2, a_sub_norm, a_w_o, kv_norm, kv_w_dkv, kv_a_norm, kv_w_ukv,
              b_attn_norm, b_w_dq, b_q_a_norm, b_w_uq, b_w_o, ffn_norm, ffn_w_in,
              ffn_conv_w, ffn_conv_b, ffn_w_out, final_norm):
    B = x.shape[0]
    meta = jnp.broadcast_to(meta_tokens[None].astype(x.dtype), (B, N_META, D_MODEL))
    h = jnp.concatenate([meta, x], axis=1)
    L = h.shape[1]
    pos = jnp.arange(L, dtype=jnp.int32)
    n_blocks = -(-L // Q_BLOCK)
    kc = chunk_ids(pos)
    qcb = chunk_ids(jnp.arange(n_blocks * Q_BLOCK, dtype=jnp.int32)).reshape(n_blocks, Q_BLOCK)

    k_nope = k_rope = v_shared = None
    for layer in range(DEPTH):
        if layer < N_A_LAYERS:
            h = diff_attn_layer(h, a_attn_norm[layer], a_w_qkv[layer], a_lambda_q1[layer],
                                a_lambda_k1[layer], a_lambda_q2[layer], a_lambda_k2[layer],
                                a_sub_norm[layer], a_w_o[layer], layer, pos, qcb, kc)
        else:
            if layer == N_A_LAYERS:
                k_nope, k_rope, v_shared = shared_latent_kv(h, kv_norm, kv_w_dkv, kv_a_norm,
                                                            kv_w_ukv, pos)
            i = layer - N_A_LAYERS
            h = mla_layer(h, b_attn_norm[i], b_w_dq[i], b_q_a_norm[i], b_w_uq[i], b_w_o[i],
                          k_nope, k_rope, v_shared, pos, qcb, kc)
        h = conv_ffn(h, ffn_norm[layer], ffn_w_in[layer], ffn_conv_w[layer],
                     ffn_conv_b[layer], ffn_w_out[layer])
    return rms_norm(h, final_norm)[:, N_META:]
```

```python
import contextlib
import numpy as np
import concourse.bass as bass
import concourse.mybir as mybir
from concourse.bass_utils import run_bass_kernel_spmd

F32 = mybir.dt.float32
BF16 = mybir.dt.bfloat16
AF = mybir.ActivationFunctionType
ALU = mybir.AluOpType
AX = mybir.AxisListType

ENGS = ["pe", "act", "dve", "pool", "sp"]
SEM_CH = 4000


class DmaGroup:
    def __init__(self, gid):
        self.gid = gid
        self.count = 0
        self.sem = None


class Prog:
    def __init__(self):
        self.ops = {e: [] for e in ENGS}
        self.last_w = {}
        self.readers = {}
        self.waited = {c: {e: -1 for e in ENGS} for c in ENGS}
        self.waited_dma = {c: {} for c in ENGS}
        self.groups = []
        self.pending = {e: [] for e in ENGS}
        self.marks = []

    def dma_group(self):
        g = DmaGroup(len(self.groups))
        self.groups.append(g)
        return g

    def op(self, eng, fn, r=(), w=(), dma=None):
        idx = len(self.ops[eng])
        o = dict(eng=eng, fn=fn, waits=[], idx=idx, sig=False, dma=None)
        deps = []
        for k in r:
            t = self.last_w.get(k)
            if t is not None:
                deps.append(t)
        for k in w:
            t = self.last_w.get(k)
            if t is not None:
                deps.append(t)
            deps.extend(self.readers.get(k, ()))
        deps.extend(self.pending[eng])
        self.pending[eng] = []
        best_e, best_d = {}, {}
        for t in deps:
            if t[0] == "e":
                if t[2] > best_e.get(t[1], -1):
                    best_e[t[1]] = t[2]
            else:
                if t[2] > best_d.get(t[1].gid, (None, 0))[1]:
                    best_d[t[1].gid] = (t[1], t[2])
        for e, i in best_e.items():
            if e == eng:
                if eng in ("pe", "sp"):
                    continue
            if i <= self.waited[eng][e]:
                continue
            self.waited[eng][e] = i
            self.ops[e][i]["sig"] = True
            o["waits"].append(("e", e, i))
        for gid, (g, v) in best_d.items():
            if v <= self.waited_dma[eng].get(gid, 0):
                continue
            self.waited_dma[eng][gid] = v
            o["waits"].append(("d", g, v))
        if dma is not None:
            dma.count += 16
            o["dma"] = dma
            tok = ("d", dma, dma.count)
        else:
            tok = ("e", eng, idx)
        for k in w:
            self.last_w[k] = tok
            self.readers[k] = []
        for k in r:
            if k in w:
                continue
            self.readers.setdefault(k, []).append(tok)
        self.ops[eng].append(o)
        return tok

    def mark(self, name):
        self.marks.append(name)
        self.op("pe", "MARK")

    def barrier(self):
        toks = []
        for e in ENGS:
            if self.ops[e]:
                for o in reversed(self.ops[e]):
                    if o["dma"] is None and o["fn"] != "MARK":
                        toks.append(("e", e, o["idx"]))
                        break
        for g in self.groups:
            if g.count:
                toks.append(("d", g, g.count))
        for e in ENGS:
            self.pending[e] = list(toks)

    def finalize(self, nc, stack):
        self.nsig = {}
        for e in ENGS:
            c = 0
            for o in self.ops[e]:
                if o["sig"]:
                    c += 1
                    o["sigval"] = c
            self.nsig[e] = c
        self.esems = {}
        for e in ENGS:
            n = (self.nsig[e] + SEM_CH - 1) // SEM_CH
            self.esems[e] = [stack.enter_context(nc.semaphore(f"s_{e}{i}")) for i in range(n)]
        for g in self.groups:
            if g.count:
                g.sem = stack.enter_context(nc.semaphore(f"d_{g.gid}"))
        self.mark_sem = stack.enter_context(nc.semaphore("phase_mark")) if self.marks else None

    def emit_engine(self, ename, eng):
        for o in self.ops[ename]:
            for t in o["waits"]:
                if t[0] == "e":
                    _, e, i = t
                    sv = self.ops[e][i]["sigval"]
                    sem = self.esems[e][(sv - 1) // SEM_CH]
                    val = (sv - 1) % SEM_CH + 1
                    eng.wait_ge(sem, val)
                else:
                    _, g, v = t
                    eng.wait_ge(g.sem, v)
            if o["fn"] == "MARK":
                eng.nop().then_inc(self.mark_sem, 1)
                continue
            ins = o["fn"](eng)
            if o["dma"] is not None:
                ins.then_inc(o["dma"].sem, 16)
            elif o["sig"]:
                sv = o["sigval"]
                ins.then_inc(self.esems[ename][(sv - 1) // SEM_CH], 1)

    def run_block(self, nc):
        with nc.Block() as block:
            @block.tensor
            def _(e):
                self.emit_engine("pe", e)

            @block.scalar
            def _(e):
                self.emit_engine("act", e)

            @block.vector
            def _(e):
                self.emit_engine("dve", e)

            @block.gpsimd
            def _(e):
                self.emit_engine("pool", e)

            @block.sync
            def _(e):
                self.emit_engine("sp", e)


D = 1024
SEQ = 2048
NMETA = 16
T = SEQ + NMETA
EPS = 1e-5
GR = [(0, 512), (512, 512), (1024, 512), (1536, 512), (2048, 16)]
TT = [(i * 128, 128) for i in range(16)] + [(2048, 16)]
DFF = 2816
FBLOCKS = [list(range(0, 8)), list(range(8, 15)), list(range(15, 22))]
LAM_INIT0 = 0.8 - 0.6 * 1.0
T6 = [0, 1, 2, 3, 4, 5]
T4 = [0, 1, 2, 3]
PRINT_MARKS = False


def build_nc(mode):
    nc = bass.Bass("TRN2", target_bir_lowering=False)
    dt_in = lambda n, s: nc.dram_tensor(n, s, F32, kind="ExternalInput").ap()
    dt_out = lambda n, s: nc.dram_tensor(n, s, F32, kind="ExternalOutput").ap()
    do0 = mode in ("ALL", "L0")
    do1 = mode in ("ALL", "L1")
    if do0:
        x = dt_in("x", [SEQ, D]); meta = dt_in("meta", [NMETA, D])
        w_qkv = dt_in("w_qkv", [D, 3072]); w_qk_sw = dt_in("w_qk_sw", [D, 2048])
        a_w_o = dt_in("a_w_o", [D, D]); lam4 = dt_in("lam4", [4, 64]); tabA = dt_in("tabA", [2, 128, T])
    if do1:
        w_dkv = dt_in("w_dkv", [D, 288]); w_dkv_r = dt_in("w_dkv_r", [D, 96]); w_dkv_rs = dt_in("w_dkv_rs", [D, 96])
        w_ukv = dt_in("w_ukv", [256, 2048]); w_dq = dt_in("w_dq", [D, 384])
        w_uq = dt_in("w_uq", [384, 1536]); w_uq_sw = dt_in("w_uq_sw", [384, 1536])
        b_w_o = dt_in("b_w_o", [D, D]); tabB = dt_in("tabB", [2, 128, T]); fin = dt_in("final_norm", [D])
    w_in = dt_in("w_in", [2, D, 2 * DFF]); w_out = dt_in("w_out", [2, DFF, D])
    vecs1 = dt_in("vecs1", [90, 128]); vecs2 = dt_in("vecs2", [66, 128]); vecs3 = dt_in("vecs3", [66, 128])
    if mode == "L0":
        hmid = dt_out("hmid", [128, 8 * T])
    if mode == "L1":
        hmid = dt_in("hmid", [128, 8 * T])
    if do1:
        out = dt_out("out", [SEQ, D])

    P = Prog()
    with contextlib.ExitStack() as st:
        sb = lambda n, s, d: st.enter_context(nc.sbuf_tensor(n, s, d))
        hT = sb("hT", [128, 8 * T], F32)
        xn = sb("xn", [128, 8 * T], BF16)
        arF = sb("arF", [128, 2 * (T + 2)], F32)
        arB = sb("arB", [128, 16736], BF16)
        tmp = [sb(f"tmp{i}", [128, 512], F32) for i in range(6)]
        wp = [sb(f"wp{i}", [128, 1024], BF16) for i in range(8)]
        ckvn = sb("ckvn", [128, 2 * T], BF16)
        pt = [sb(f"pt{i}", [128, 512], BF16) for i in range(4)]
        sq = [sb(f"sq{i}", [128, 512], BF16) for i in range(4)]
        rs = [sb(f"rs{i}", [128, 512], F32) for i in range(2)]
        sqe = [sb(f"sqe{i}", [128, 512], BF16) for i in range(2)]
        ocm = [sb(f"ocm{i}", [128, 512], F32) for i in range(2)]
        gfin = sb("gfin", [128, 1024], F32)
        ident = sb("ident", [128, 128], F32)
        ones = sb("ones", [128, 128], BF16)
        VT1 = sb("VT1", [128, 90], F32)
        VT2 = sb("VT2", [128, 66], F32)
        VT3 = sb("VT3", [128, 66], F32)
        cst = sb("cst", [128, 8], F32)
        lamb = sb("lamb", [128, 272], F32)
        ps = [st.enter_context(nc.psum_tensor(f"ps{i}", [128, 512], F32)) for i in range(8)]

        hT3 = hT[:, :].rearrange("p (k t) -> p k t", k=8)
        xn3 = xn[:, :].rearrange("p (k t) -> p k t", k=8)
        ckvn3 = ckvn[:, :].rearrange("p (k t) -> p k t", k=2)
        TC = arF[:, 0:T]
        TS_ = arF[:, T + 2:2 * T + 2]
        AR = [arF[:, 0:T + 2], arF[:, T + 2:2 * T + 4]]
        QB = [arB[:, 0:T], arB[:, T:2 * T]]
        KB = [arB[:, 2 * T:3 * T], arB[:, 3 * T:4 * T]]
        VB = [arB[:, 4 * T + i * 2176:4 * T + (i + 1) * 2176].rearrange("p (t c) -> p t c", c=128) for i in range(2)]
        oTb = [arB[:, 4 * T + 4352 + i * T:4 * T + 4352 + (i + 1) * T] for i in range(2)]
        gated3 = arB[:, 0:8 * T].rearrange("p (j t) -> p j t", j=8)
        epsc = cst[:, 0:1]

        def psk(b):
            return ("ps", b)

        def MM(o, lhsT, rhs, start, stop, r=(), w=()):
            P.op("pe", lambda e: e.matmul(o, lhsT, rhs, start=start, stop=stop), r=r, w=w)

        def ACT(o, i, func, r=(), w=(), scale=1.0, bias=None):
            if bias is None:
                P.op("act", lambda e: e.activation(out=o, in_=i, func=func, scale=scale), r=r, w=w)
            else:
                P.op("act", lambda e: e.activation(out=o, in_=i, func=func, scale=scale, bias=bias), r=r, w=w)

        def TTo(eng, o, a, b, op, r=(), w=()):
            P.op(eng, lambda e: e.tensor_tensor(out=o, in0=a, in1=b, op=op), r=r, w=w)

        def STT(eng, o, a, s, b, op0, op1, r=(), w=()):
            P.op(eng, lambda e: e.scalar_tensor_tensor(out=o, in0=a, scalar=s, in1=b, op0=op0, op1=op1), r=r, w=w)

        def TS(eng, o, a, s1, s2, op0, op1, r=(), w=()):
            P.op(eng, lambda e: e.tensor_scalar(out=o, in0=a, scalar1=s1, scalar2=s2, op0=op0, op1=op1), r=r, w=w)

        def RECIP(o, i, r=(), w=()):
            nf = o.shape[-1]
            if nf < 64:
                P.op("dve", lambda e: e.reciprocal(out=o, in_=i), r=r, w=w)
                return
            ACT(o, i, AF.Ln, r=r, w=w)
            ACT(o, o, AF.Exp, w=[k for k in w if k[0] != "ps"], scale=-1.0)

        def RSTD(o, i, scale, r=(), w=()):
            ACT(o, i, AF.Ln, r=list(r) + ["cst"], w=w, scale=scale, bias=epsc)
            ACT(o, o, AF.Exp, w=[k for k in w if k[0] != "ps"], scale=-0.5)

        def COPY(eng, o, i, r=(), w=()):
            if eng == "act":
                P.op("act", lambda e: e.activation(out=o, in_=i, func=AF.Copy), r=r, w=w)
            else:
                P.op(eng, lambda e: e.tensor_copy(out=o, in_=i), r=r, w=w)

        def MEMSET(eng, o, val, r=(), w=()):
            P.op(eng, lambda e: e.memset(o, val), r=r, w=w)

        def DMA(q, o, i, grp, r=(), w=()):
            P.op(q, lambda e: e.dma_start(out=o, in_=i), r=r, w=w, dma=grp)

        wgrp = [P.dma_group() for _ in range(8)]
        wctr = [0]

        def LOADW(view_fn, src):
            i = wctr[0] % 8
            wctr[0] += 1
            v = view_fn(wp[i])
            DMA("pool", v, src, wgrp[i], w=[("w", i)])
            return v, ("w", i)

        def kview(kt, m):
            return lambda buf: buf[:, 0:kt * m].rearrange("p (k m) -> p k m", k=kt)

        def ksrc(ap2d):
            return ap2d.rearrange("(k p) m -> p k m", p=128)

        rot = {}

        def nxt(name, lst):
            i = rot.get(name, 0)
            rot[name] = i + 1
            return lst[i % len(lst)]

        gmisc = P.dma_group()
        gtab = P.dma_group()
        gx = [P.dma_group() for _ in range(4)]
        gout = [P.dma_group() for _ in range(2)]

        MEMSET("pool", ident[:, :], 0.0, w=["ident"])
        P.op("pool", lambda e: e.affine_select(out=ident[:, :], in_=ident[:, :], pattern=[[-1, 128]],
                                                compare_op=ALU.not_equal, fill=1.0, base=0, channel_multiplier=1),
             r=["ident"], w=["ident"])
        MEMSET("pool", ones[:, :], 1.0, w=["ones"])
        MEMSET("pool", cst[:, 0:1], EPS, w=["cst"])
        MEMSET("pool", cst[0:64, 1:2], 0.0, w=["cst"])
        MEMSET("pool", cst[64:128, 1:2], -30000.0, w=["cst"])
        vsrc = ((vecs1, 90, VT1, "VT1"), (vecs2, 66, VT2, "VT2"), (vecs3, 66, VT3, "VT3"))
        gv = [P.dma_group() for _ in range(3)]
        for vi, (src, R, dst, nm) in enumerate(vsrc):
            DMA("sp", tmp[vi][0:R, 0:128], src[:, :], gv[vi], w=[("tmp", vi)])
        for vi, (src, R, dst, nm) in enumerate(vsrc):
            stg = tmp[vi][0:R, 0:128]
            P.op("pe", lambda e, stg=stg, R=R, vi=vi: e.transpose(ps[5 + vi][:, 0:R], stg, ident[0:R, 0:R]),
                 r=[("tmp", vi), "ident"], w=[psk(5 + vi)])
            COPY("dve", dst[:, :], ps[5 + vi][:, 0:R], w=[psk(5 + vi), nm])
        VG = {"a_attn": 0, "ffn0": 8, "kv": 16, "b_attn": 24, "ffn1": 32, "kv_a": 40, "q_a": 42, "sub": 45,
              "cb0": 46, "cb1": 68}
        ALLHT = [("hT", k, g) for k in range(8) for g in range(5)]

        if do0:
            xs = [arF[:, i * 1024:(i + 1) * 1024] for i in range(4)]
            for g in range(4):
                for t4 in range(4):
                    tt = g * 4 + t4
                    DMA("sp", xs[t4], x[tt * 128:(tt + 1) * 128, :], gx[t4], w=[("xs", t4)])
                    for k in range(8):
                        P.op("pe", lambda e, k=k, t4=t4: e.transpose(ps[k][:, t4 * 128:(t4 + 1) * 128],
                                                                     xs[t4][:, k * 128:(k + 1) * 128], ident[:, :]),
                             r=[("xs", t4), "ident"], w=[psk(k)])
                for k in range(8):
                    COPY("act" if k % 2 else "dve", hT3[:, k, g * 512:(g + 1) * 512], ps[k][:, :],
                         w=[psk(k), ("hT", k, g)])
            DMA("sp", xs[0][0:16, :], meta[:, :], gx[0], w=[("xs", 0)])
            for k in range(8):
                P.op("pe", lambda e, k=k: e.transpose(ps[k][:, 0:16], xs[0][0:16, k * 128:(k + 1) * 128], ident[0:16, 0:16]),
                     r=[("xs", 0), "ident"], w=[psk(k)])
                COPY("act" if k % 2 else "dve", hT3[:, k, 2048:2064], ps[k][:, 0:16], w=[psk(k), ("hT", k, 4)])
        else:
            DMA("sp", hT[:, :], hmid[:, :], gmisc, w=ALLHT)
        P.barrier()

        def norm_phase(gcol, after=None):
            P.mark(f"norm{gcol}")
            for g, (s0, n) in enumerate(GR):
                nb = nxt("pN", [6, 7])
                for k in range(8):
                    sqi = nxt("sq", [0, 1, 2, 3])
                    ACT(sq[sqi][:, :n], hT3[:, k, s0:s0 + n], AF.Square, r=[("hT", k, g)], w=[("sq", sqi)])
                    MM(ps[nb][:, :n], ones[:, :], sq[sqi][:, :n], k == 0, k == 7, r=["ones", ("sq", sqi)], w=[psk(nb)])
                ri = nxt("rs", [0, 1])
                RSTD(rs[ri][:, :n], ps[nb][:, :n], 1.0 / D, w=[psk(nb), ("rs", ri)])
                for k in range(8):
                    STT("dve", xn3[:, k, s0:s0 + n], hT3[:, k, s0:s0 + n], VT1[:, gcol + k:gcol + k + 1], rs[ri][:, :n],
                        ALU.mult, ALU.mult, r=[("hT", k, g), ("rs", ri), "VT1"], w=[("xn", k, g)])
                if after is not None:
                    after(g)

        def _pv(pd, ob, sbk, n, nk, Vv, vkey, extra_r=()):
            pti, kt, kn, qlo, idx = pd
            MM(ps[ob][:, qlo:n], Vv[:kn, kt, :], ptl[pti][:kn, qlo:n], idx == 0, idx == nk - 1,
               r=[("pt", pti), vkey] + list(extra_r), w=[psk(ob)])
            if sbk is not None:
                MM(ps[sbk][:, qlo:n], ones[:kn, :], ptl[pti][:kn, qlo:n], idx == 0, idx == nk - 1,
                   r=[("pt", pti), "ones"], w=[psk(sbk)])

        CHUNK = 2
        chunk_cfg = [2]
        ptl = [pt[0], pt[1], pt[2], pt[3]]
        SBANKS = [0, 1, 6, 7]
        obanks = [[2, 3, 4, 5]]

        def attention_unit(maps, scale, evac_map, combine, sep_sums, fillers=(), carry=None, last=True):
            fillers = list(fillers)
            deferred = list(carry) if carry else []
            chain = [0]

            def run_deferred(item):
                d = item[1]()
                if d is not None:
                    deferred.insert(0, [item[0], d])

            def pop_filler():
                f = fillers.pop(0)
                if getattr(f, "needs_flush", False):
                    while deferred and deferred[0][0] == 0:
                        run_deferred(deferred.pop(0))
                f()

            CH = chunk_cfg[0]
            steps_left = [len(maps) * sum(-(-(5 + 4 * g) // CH) - 1 for g in range(4))]
            for g in [4, 0, 1, 2, 3]:
                s0, n = GR[g]
                kts = [16] if g == 4 else [16] + list(range(0, 4 * g + 4))
                nk = len(kts)
                chunks = [list(range(i, min(i + CH, nk))) for i in range(0, nk, CH)]
                for mi, (Kb, Qb, r0, nr, kkeys, qkey, Vv, vkey) in enumerate(maps):
                    chain[0] += 1
                    while deferred and deferred[0][0] < chain[0] - 1:
                        run_deferred(deferred.pop(0))
                    ob = nxt("pO", obanks[0])
                    sbk = nxt("pO", obanks[0]) if sep_sums else None

                    def emit_pv(chunk, ob=ob, sbk=sbk, n=n, nk=nk, Vv=Vv, vkey=vkey):
                        keys = [("pt", c[0]) for c in chunk] + [("ptm", c[0]) for c in chunk]
                        for ci, c in enumerate(chunk):
                            _pv(c, ob, sbk, n, nk, Vv, vkey, extra_r=keys if ci == 0 else ())

                    prev = None
                    for ch in chunks:
                        cur = []
                        for idx in ch:
                            kt = kts[idx]
                            ks0, kn = TT[kt]
                            qlo = 0
                            diag = g < 4 and kt != 16 and kt >= 4 * g
                            if diag:
                                qlo = 128 * (kt - 4 * g)
                            sbank = nxt("pS", SBANKS)
                            pti = nxt("pt", list(range(len(ptl))))
                            MM(ps[sbank][:kn, qlo:n], Kb[r0:r0 + nr, ks0:ks0 + kn], Qb[r0:r0 + nr, s0 + qlo:s0 + n], True, True,
                               r=list(kkeys) + [qkey(g)], w=[psk(sbank)])
                            if diag:
                                ACT(ptl[pti][:kn, qlo + 64:n], ps[sbank][:kn, qlo + 64:n], AF.Exp, r=[psk(sbank)],
                                    w=[("pt", pti)], scale=scale)
                                ACT(ptl[pti][:kn, qlo:qlo + 64], ps[sbank][:kn, qlo:qlo + 64], AF.Exp, r=[psk(sbank), "cst"],
                                    w=[("ptm", pti)], scale=scale, bias=cst[:kn, 1:2])
                            else:
                                ACT(ptl[pti][:kn, qlo:n], ps[sbank][:kn, qlo:n], AF.Exp, r=[psk(sbank)],
                                    w=[("pt", pti), ("ptm", pti)], scale=scale)
                            cur.append((pti, kt, kn, qlo, idx))
                        if prev is None:
                            for _ in range(min(2, len(fillers))):
                                pop_filler()
                        if prev is not None:
                            emit_pv(prev)
                            steps_left[0] -= 1
                            if deferred:
                                run_deferred(deferred.pop(0))
                            npop = -(-len(fillers) // max(steps_left[0] + 1, 1))
                            for _ in range(min(npop, len(fillers))):
                                pop_filler()
                        prev = cur
                    emit_pv(prev)
                    deferred.append([chain[0], (lambda mi=mi, g=g, s0=s0, n=n, ob=ob, sbk=sbk: evac_map(mi, g, s0, n, ob, sbk))])
                if combine is not None:
                    deferred.append([chain[0], (lambda g=g, s0=s0, n=n: combine(g, s0, n))])
            while fillers:
                pop_filler()
            if last:
                while deferred:
                    run_deferred(deferred.pop(0))
                return []
            return [[0, d[1]] for d in deferred]

        def wo_tiles(wo, wk, oT_slot, okeys):
            tiles = []
            for g in [4, 0, 1, 2, 3]:
                s0, n = GR[g]
                for m in range(8):
                    def f(m=m, g=g, s0=s0, n=n):
                        b = nxt("pS", SBANKS)
                        MM(ps[b][:, :n], wo[:, m * 128:(m + 1) * 128], oTb[oT_slot][:, s0:s0 + n], True, True,
                           r=[wk] + okeys(g), w=[psk(b)])
                        TTo("dve", hT3[:, m, s0:s0 + n], ps[b][:, :n], hT3[:, m, s0:s0 + n], ALU.add,
                            w=[psk(b), ("hT", m, g)])
                    f.needs_flush = True
                    tiles.append(f)
            return tiles

        def proj_rot(dst, wfn, wk, wsfn, wsk, ktiles, src3, srckey, rows, dkey, lazy=False):
            out = []
            for g, (s0, n) in enumerate(GR):
                def f(g=g, s0=s0, n=n):
                    if lazy:
                        b1 = nxt("pS", SBANKS)
                        b2 = nxt("pS", SBANKS)
                    else:
                        b1, b2 = nxt("pP", [(0, 1), (6, 7)])
                    for k in range(ktiles):
                        MM(ps[b1][0:rows, :n], wfn(k), src3[:, k, s0:s0 + n], k == 0, k == ktiles - 1,
                           r=[wk, (srckey, k, g)], w=[psk(b1)])
                    for k in range(ktiles):
                        MM(ps[b2][0:rows, :n], wsfn(k), src3[:, k, s0:s0 + n], k == 0, k == ktiles - 1,
                           r=[wsk, (srckey, k, g)], w=[psk(b2)])
                    t1 = nxt("tmp", T4)
                    t2 = nxt("tmp", T4)
                    TTo("dve", tmp[t1][0:rows, :n], ps[b1][0:rows, :n], TC[0:rows, s0:s0 + n], ALU.mult,
                        r=["tab"], w=[psk(b1), ("tmp", t1)])
                    TTo("dve", tmp[t2][0:rows, :n], ps[b2][0:rows, :n], TS_[0:rows, s0:s0 + n], ALU.mult,
                        r=["tab"], w=[psk(b2), ("tmp", t2)])
                    TTo("pool" if lazy else "dve", dst[0:rows, s0:s0 + n], tmp[t1][0:rows, :n], tmp[t2][0:rows, :n], ALU.add,
                        r=[("tmp", t1), ("tmp", t2)], w=[dkey(g)])
                if lazy:
                    out.append(f)
                else:
                    f()
            return out

        def interleave(a, b):
            out = []
            na, nb_ = len(a), len(b)
            ia = ib = 0
            while ia < na or ib < nb_:
                if ib >= nb_ or (ia < na and ia * nb_ <= ib * na):
                    out.append(a[ia]); ia += 1
                else:
                    out.append(b[ib]); ib += 1
            return out


        def layer_A():
            DMA("sp", TC, tabA[0], gtab, w=["tab"])
            DMA("sp", TS_, tabA[1], gtab, w=["tab"])
            for i in range(4):
                DMA("sp", lamb[:, i * 64:(i + 1) * 64], lam4[i].partition_broadcast(128), gmisc, w=["lamb"])
            def load_proj(h):
                c = h * 128
                W = {}
                for nm, c0 in (("Q", c), ("K", 1024 + c)):
                    W[nm] = LOADW(kview(8, 128), ksrc(w_qkv[:, c0:c0 + 128]))
                    W[nm + "s"] = LOADW(kview(8, 128), ksrc(w_qk_sw[:, c0:c0 + 128]))
                W["V"] = LOADW(kview(8, 128), ksrc(w_qkv[:, 2048 + c:2048 + c + 128]))
                return W

            def proj_unit(h, W, lazy):
                slot = h % 2
                Qb, Kb, Vv = QB[slot], KB[slot], VB[slot]
                out = []
                for (dst, nm) in ((Qb, "Q"), (Kb, "K")):
                    (wv, wk), (wsv, wsk) = W[nm], W[nm + "s"]
                    out += proj_rot(dst, (lambda k, wv=wv: wv[:, k, :]), wk, (lambda k, wsv=wsv: wsv[:, k, :]), wsk,
                                    8, xn3, "xn", 128, (lambda g, nm=nm, slot=slot: (nm + "f", slot, g)), lazy=lazy)
                wvv, wvk = W["V"]
                vkey = ("Vf", slot)
                for t0 in range(0, 17, 4):
                    def fv(t0=t0):
                        vb = nxt("pS", SBANKS) if lazy else nxt("pO", [2, 3, 4, 5])
                        for tt in range(t0, min(t0 + 4, 17)):
                            s0, n = TT[tt]
                            c0 = (tt % 4) * 128
                            g = min(tt // 4, 4)
                            for k in range(8):
                                MM(ps[vb][:n, c0:c0 + 128], xn3[:, k, s0:s0 + n], wvv[:, k, :], k == 0, k == 7,
                                   r=[wvk, ("xn", k, g)], w=[psk(vb)])
                        if t0 < 16:
                            COPY("act", Vv[:, t0:t0 + 4, :], ps[vb][:, :].rearrange("p (t c) -> p t c", c=128),
                                 w=[psk(vb), vkey])
                        else:
                            COPY("act", Vv[0:16, 16, :], ps[vb][0:16, 0:128], w=[psk(vb), vkey])
                    if lazy:
                        out.append(fv)
                    else:
                        fv()
                return out

            W0 = load_proj(0)
            first = proj_unit(0, W0, True)
            norm_phase(VG["a_attn"], after=lambda g: (first[g](), first[5 + g](), first[10 + g]()))
            TTo("dve", lamb[:, 0:64], lamb[:, 0:64], lamb[:, 64:128], ALU.mult, w=["lamb"])
            TTo("dve", lamb[:, 128:192], lamb[:, 128:192], lamb[:, 192:256], ALU.mult, w=["lamb"])
            P.op("dve", lambda e: e.reduce_sum(out=lamb[:, 256:257], in_=lamb[:, 0:64], axis=AX.X), w=["lamb"])
            P.op("dve", lambda e: e.reduce_sum(out=lamb[:, 257:258], in_=lamb[:, 128:192], axis=AX.X), w=["lamb"])
            ACT(lamb[:, 258:260], lamb[:, 256:258], AF.Exp, w=["lamb"])
            TTo("dve", lamb[:, 260:261], lamb[:, 258:259], lamb[:, 259:260], ALU.subtract, w=["lamb"])
            TS("dve", lamb[:, 261:262], lamb[:, 260:261], LAM_INIT0, -1.0, ALU.add, ALU.mult, w=["lamb"])
            TS("dve", lamb[:, 262:263], VT1[:, VG["sub"]:VG["sub"] + 1], 1.0 - LAM_INIT0, 0.0, ALU.mult, ALU.add,
               r=["VT1"], w=["lamb"])
            neglam = lamb[:, 261:262]
            gsub = lamb[:, 262:263]


            P.mark("A0.proj")
            prev_fill = []
            carryA = []
            for h in range(8):
                slot = h % 2
                Qb, Kb, Vv = QB[slot], KB[slot], VB[slot]
                vkey = ("Vf", slot)
                next_proj = []
                if h + 1 < 8:
                    next_proj = proj_unit(h + 1, load_proj(h + 1), True)
                wo, wok = LOADW(lambda b: b[:, :], a_w_o[h * 128:(h + 1) * 128, :])
                this_fill = wo_tiles(wo, wok, slot, (lambda g, slot=slot: [("oT", slot, g)]))
                prev_fill = interleave(prev_fill, next_proj)

                o_t = {}

                def evac_map(mi, g, s0, n, ob, sbk, o_t=o_t):
                    to = 4 + mi
                    RECIP(tmp[to][:, :n], ps[sbk][:, :n], w=[psk(sbk), ("tmp", to)])
                    TTo("dve", tmp[to][:, :n], ps[ob][:, :n], tmp[to][:, :n], ALU.mult, w=[psk(ob), ("tmp", to)])
                    o_t[mi] = to

                def combine(g, s0, n, slot=slot, o_t=o_t):
                    t1, t2 = o_t[0], o_t[1]
                    STT("dve", tmp[t1][:, :n], tmp[t2][:, :n], neglam, tmp[t1][:, :n], ALU.mult, ALU.add,
                        r=[("tmp", t2), "lamb"], w=[("tmp", t1)])
                    sqi = nxt("sqe", [0, 1])
                    ACT(sqe[sqi][:, :n], tmp[t1][:, :n], AF.Square, r=[("tmp", t1)], w=[("sqe", sqi)])
                    oi = nxt("ocm", [0, 1])
                    COPY("pool", ocm[oi][:, :n], tmp[t1][:, :n], r=[("tmp", t1)], w=[("ocm", oi)])

                    def tail(g=g, s0=s0, n=n, sqi=sqi, oi=oi, slot=slot):
                        nb = nxt("pS", SBANKS)
                        MM(ps[nb][:, :n], ones[:, :], sqe[sqi][:, :n], True, True, r=["ones", ("sqe", sqi)], w=[psk(nb)])
                        ri = nxt("rs", [0, 1])
                        RSTD(rs[ri][:, :n], ps[nb][:, :n], 1.0 / 128, w=[psk(nb), ("rs", ri)])
                        STT("dve", oTb[slot][:, s0:s0 + n], ocm[oi][:, :n], gsub, rs[ri][:, :n], ALU.mult, ALU.mult,
                            r=[("ocm", oi), ("rs", ri), "lamb"], w=[("oT", slot, g)])
                    return tail

                P.mark(f"A{h}.attn")
                kk = [("Kf", slot, g) for g in range(5)]
                qk = lambda g, slot=slot: ("Qf", slot, g)
                carryA = attention_unit([(Kb, Qb, 0, 64, kk, qk, Vv, vkey), (Kb, Qb, 64, 64, kk, qk, Vv, vkey)], 64 ** -0.5,
                                        evac_map, combine, True, fillers=prev_fill, carry=carryA, last=(h == 7))
                prev_fill = this_fill
            P.mark("A.wo_last")
            for f in prev_fill:
                f()

        def layer_B():
            DMA("sp", TC, tabB[0], gtab, w=["tab"])
            DMA("sp", TS_, tabB[1], gtab, w=["tab"])
            wl0, wl0k = LOADW(kview(8, 128), ksrc(w_dkv[:, 0:128]))
            wl1, wl1k = LOADW(kview(8, 128), ksrc(w_dkv[:, 128:256]))
            wr, wrk = LOADW(kview(8, 96), ksrc(w_dkv_r[:, :]))
            wrs, wrsk = LOADW(kview(8, 96), ksrc(w_dkv_rs[:, :]))
            DMA("sp", gfin[:, :], fin.partition_broadcast(128), gmisc, w=["gfin"])
            def kv_group(g):
                s0, n = GR[g]
                cb = [nxt("pO", [2, 3, 4, 5]) for _ in range(2)]
                for j, (wl, wlk) in enumerate(((wl0, wl0k), (wl1, wl1k))):
                    for k in range(8):
                        MM(ps[cb[j]][:, :n], wl[:, k, :], xn3[:, k, s0:s0 + n], k == 0, k == 7,
                           r=[wlk, ("xn", k, g)], w=[psk(cb[j])])
                nb = nxt("pN", [6, 7])
                for j in range(2):
                    sqi = nxt("sq", [0, 1, 2, 3])
                    ACT(sq[sqi][:, :n], ps[cb[j]][:, :n], AF.Square, w=[psk(cb[j]), ("sq", sqi)])
                    MM(ps[nb][:, :n], ones[:, :], sq[sqi][:, :n], j == 0, j == 1, r=["ones", ("sq", sqi)], w=[psk(nb)])
                ri = nxt("rs", [0, 1])
                RSTD(rs[ri][:, :n], ps[nb][:, :n], 1.0 / 256, w=[psk(nb), ("rs", ri)])
                for j in range(2):
                    STT("dve", ckvn3[:, j, s0:s0 + n], ps[cb[j]][:, :n], VT1[:, VG["kv_a"] + j:VG["kv_a"] + j + 1],
                        rs[ri][:, :n], ALU.mult, ALU.mult, r=[("rs", ri), "VT1"], w=[psk(cb[j]), ("ckvn", j, g)])
                b1, b2 = nxt("pP", [(0, 1)])
                for k in range(8):
                    MM(ps[b1][0:96, :n], wr[:, k, :], xn3[:, k, s0:s0 + n], k == 0, k == 7, r=[wrk, ("xn", k, g)], w=[psk(b1)])
                for k in range(8):
                    MM(ps[b2][0:96, :n], wrs[:, k, :], xn3[:, k, s0:s0 + n], k == 0, k == 7, r=[wrsk, ("xn", k, g)], w=[psk(b2)])
                t1 = nxt("tmp", T6)
                t2 = nxt("tmp", T6)
                TTo("dve", tmp[t1][64:96, :n], ps[b1][64:96, :n], TC[64:96, s0:s0 + n], ALU.mult,
                    r=["tab"], w=[psk(b1), ("tmp", t1)])
                TTo("dve", tmp[t2][64:96, :n], ps[b2][64:96, :n], TS_[64:96, s0:s0 + n], ALU.mult,
                    r=["tab"], w=[psk(b2), ("tmp", t2)])
                for hh in range(2):
                    TTo("pool", KB[hh][64:96, s0:s0 + n], tmp[t1][64:96, :n], tmp[t2][64:96, :n], ALU.add,
                        r=[("tmp", t1), ("tmp", t2)], w=[("krope", hh, g)])
            norm_phase(VG["kv"], after=kv_group)
            P.mark("B.kv")
            wq = [LOADW(kview(8, 128), ksrc(w_dq[:, j * 128:(j + 1) * 128])) for j in range(3)]
            def dq_group(g):
                s0, n = GR[g]
                cb = [nxt("pO", [2, 3, 4, 5]) for _ in range(3)]
                for j in range(3):
                    for k in range(8):
                        MM(ps[cb[j]][:, :n], wq[j][0][:, k, :], xn3[:, k, s0:s0 + n], k == 0, k == 7,
                           r=[wq[j][1], ("xn", k, g)], w=[psk(cb[j])])
                nb = nxt("pN", [6, 7])
                for j in range(3):
                    sqi = nxt("sq", [0, 1, 2, 3])
                    ACT(sq[sqi][:, :n], ps[cb[j]][:, :n], AF.Square, w=[psk(cb[j]), ("sq", sqi)])
                    MM(ps[nb][:, :n], ones[:, :], sq[sqi][:, :n], j == 0, j == 2, r=["ones", ("sq", sqi)], w=[psk(nb)])
                ri = nxt("rs", [0, 1])
                RSTD(rs[ri][:, :n], ps[nb][:, :n], 1.0 / 384, w=[psk(nb), ("rs", ri)])
                for j in range(3):
                    STT("dve", xn3[:, j, s0:s0 + n], ps[cb[j]][:, :n], VT1[:, VG["q_a"] + j:VG["q_a"] + j + 1],
                        rs[ri][:, :n], ALU.mult, ALU.mult, r=[("rs", ri), "VT1"],
                        w=[psk(cb[j])] + [("xn", kk_, g) for kk_ in range(8)])
            MEMSET("pool", VB[0][:, :, 64:128], 1.0, w=[("Vf", 0)])
            MEMSET("pool", VB[1][:, :, 0:64], 1.0, w=[("Vf", 1)])

            norm_phase(VG["b_attn"], after=dq_group)
            P.mark("B.dq")

            def load_proj_B(u):
                W = {}
                W["q"] = LOADW(kview(3, 192), ksrc(w_uq[:, u * 192:(u + 1) * 192]))
                W["qs"] = LOADW(kview(3, 192), ksrc(w_uq_sw[:, u * 192:(u + 1) * 192]))
                W["kv"] = LOADW(kview(2, 256), ksrc(w_ukv[:, u * 256:(u + 1) * 256]))
                return W

            P.barrier()
            Qsets = [[QB[0], QB[1]], [xn3[:, 3, :], xn3[:, 4, :]]]
            Ksets = [[KB[0], KB[1]], [xn3[:, 5, :], xn3[:, 6, :]]]
            for hh in range(2):
                COPY("dve" if hh else "act", Ksets[1][hh][64:96, :], KB[hh][64:96, :], r=[("krope", hh, g) for g in range(5)],
                     w=[("krope2", hh)])

            def proj_B(u, W, lazy):
                st_ = u % 2
                (wq2, wq2k), (wq2s, wq2sk), (wkv2, wkv2k) = W["q"], W["qs"], W["kv"]
                out = []
                for hh in range(2):
                    out += proj_rot(Qsets[st_][hh], (lambda k, hh=hh, wq2=wq2: wq2[:, k, hh * 96:(hh + 1) * 96]), wq2k,
                                    (lambda k, hh=hh, wq2s=wq2s: wq2s[:, k, hh * 96:(hh + 1) * 96]), wq2sk,
                                    3, xn3, "xn", 96, (lambda g, hh=hh, st_=st_: ("Qf", st_, hh, g)), lazy=lazy)
                    for g, (s0, n) in enumerate(GR):
                        def fk(hh=hh, g=g, s0=s0, n=n):
                            b = nxt("pS", SBANKS) if lazy else nxt("pO", [2, 3, 4, 5])
                            for k in range(2):
                                MM(ps[b][0:64, :n], wkv2[:, k, hh * 128:hh * 128 + 64], ckvn3[:, k, s0:s0 + n], k == 0, k == 1,
                                   r=[wkv2k, ("ckvn", k, g)], w=[psk(b)])
                            COPY("dve", Ksets[st_][hh][0:64, s0:s0 + n], ps[b][0:64, :n], w=[psk(b), ("Kf", st_, hh, g)])
                        if lazy:
                            out.append(fk)
                        else:
                            fk()
                return out

            def vproj_B(W):
                wkv2, wkv2k = W["kv"]
                wv2 = wkv2.rearrange("p k (h c) -> p k h c", h=2)
                for tt, (s0, n) in enumerate(TT):
                    if tt % 4 == 0:
                        vb = nxt("pS", SBANKS)
                    c0 = (tt % 4) * 128
                    g = min(tt // 4, 4)
                    for k in range(2):
                        MM(ps[vb][:n, c0:c0 + 128].rearrange("p (h c) -> p h c", h=2), ckvn3[:, k, s0:s0 + n],
                           wv2[:, k, :, 64:128], k == 0, k == 1, r=[wkv2k, ("ckvn", k, g)], w=[psk(vb)])
                    pv3 = ps[vb][:, :].rearrange("p (t c) -> p t c", c=128)
                    if tt % 4 == 3:
                        COPY("dve", VB[0][:, tt - 3:tt + 1, 0:64], pv3[:, :, 0:64], w=[psk(vb), ("Vf", 0)])
                        COPY("dve", VB[1][:, tt - 3:tt + 1, 64:128], pv3[:, :, 64:128], w=[psk(vb), ("Vf", 1)])
                    elif tt == 16:
                        COPY("act", VB[0][0:16, 16, 0:64], ps[vb][0:16, 0:64], w=[psk(vb), ("Vf", 0)])
                        COPY("act", VB[1][0:16, 16, 64:128], ps[vb][0:16, 64:128], w=[psk(vb), ("Vf", 1)])

            P.mark("B0.proj")
            Wcur = load_proj_B(0)
            proj_B(0, Wcur, False)
            obanks[0] = [2, 3]
            SBANKS.extend([4, 5])
            ptl.extend([sq[0], sq[1]])
            chunk_cfg[0] = 3
            prev_fill = []
            carryB = []
            for u in range(8):
                slot = u % 2
                P.mark(f"B{u}.vproj")
                vproj_B(Wcur)
                next_proj = []
                if u + 1 < 8:
                    Wcur = load_proj_B(u + 1)
                    next_proj = proj_B(u + 1, Wcur, True)
                wo, wok = LOADW(lambda b: b[:, :], b_w_o[u * 128:(u + 1) * 128, :])
                this_fill = wo_tiles(wo, wok, slot, (lambda g, slot=slot: [("oTh", slot, g, 0), ("oTh", slot, g, 1)]))
                prev_fill = interleave(prev_fill, next_proj)

                def evac_map(hh, g, s0, n, ob, sbk, slot=slot):
                    r0 = hh * 64
                    q0 = 64 - r0
                    tr = nxt("tmpe", [4, 5])
                    RECIP(tmp[tr][r0:r0 + 64, :n], ps[ob][q0:q0 + 64, :n], w=[psk(ob), ("tmp", tr)])
                    TTo("dve", oTb[slot][r0:r0 + 64, s0:s0 + n], ps[ob][r0:r0 + 64, :n], tmp[tr][r0:r0 + 64, :n],
                        ALU.mult, r=[("tmp", tr)], w=[psk(ob), ("oTh", slot, g, hh)])

                P.mark(f"B{u}.attn")
                maps = []
                for hh in range(2):
                    kk = [("Kf", slot, hh, g) for g in range(5)]
                    kk += [("krope", hh, g) for g in range(5)] if slot == 0 else [("krope2", hh)]
                    maps.append((Ksets[slot][hh], Qsets[slot][hh], 0, 96, kk,
                                 (lambda g, hh=hh, slot=slot: ("Qf", slot, hh, g)), VB[hh], ("Vf", hh)))
                carryB = attention_unit(maps, 96 ** -0.5, evac_map, None, False, fillers=prev_fill, carry=carryB, last=(u == 7))
                prev_fill = this_fill
            P.mark("B.wo_last")
            for f in prev_fill:
                f()

        def ffn(l):
            VW = VT2 if l == 0 else VT3
            vwk = "VT2" if l == 0 else "VT3"
            cbcol = VG["cb0"] if l == 0 else VG["cb1"]
            def load_tile(j):
                return (LOADW(kview(8, 128), ksrc(w_in[l, :, j * 128:(j + 1) * 128])),
                        LOADW(kview(8, 128), ksrc(w_in[l, :, DFF + j * 128:DFF + (j + 1) * 128])))

            pre_ahead = [load_tile(FBLOCKS[0][0]), load_tile(FBLOCKS[0][1])]
            norm_phase(VG["ffn0"] if l == 0 else VG["ffn1"])
            for i in range(2):
                MEMSET("pool", AR[i][:, 0:2], 0.0, w=[("ARz", i)])
            order = [4, 0, 1, 2, 3]
            pend_gate = [None]
            for bi, blk in enumerate(FBLOCKS):
                P.mark(f"F{l}.in{bi}")
                nj = len(blk)
                if bi == 0:
                    ahead = pre_ahead
                else:
                    ahead = [load_tile(blk[0])] + ([load_tile(blk[1])] if nj > 1 else [])
                for jj, j in enumerate(blk):
                    (wa, wak), (wb, wbk) = ahead.pop(0)
                    if jj + 2 < nj:
                        ahead.append(load_tile(blk[jj + 2]))
                    ai = j % 2
                    A = AR[ai]
                    w0 = VW[:, 0 * 22 + j:0 * 22 + j + 1]
                    w1 = VW[:, 1 * 22 + j:1 * 22 + j + 1]
                    w2 = VW[:, 2 * 22 + j:2 * 22 + j + 1]
                    cbs = VT1[:, cbcol + j:cbcol + j + 1]
                    prevkey = ("ARz", ai)
                    for g in order:
                        s0, n = GR[g]
                        c0 = (0 if g == 4 else 16 + 512 * g) + 2
                        ba, bb = nxt("pF4", [(0, 1), (2, 3), (4, 5), (6, 7)])
                        for k in range(8):
                            MM(ps[ba][:, :n], wa[:, k, :], xn3[:, k, s0:s0 + n], k == 0, k == 7,
                               r=[wak, ("xn", k, g)] + ([wbk] if k == 0 else []), w=[psk(ba)] + ([psk(bb)] if k == 0 else []))
                        for k in range(8):
                            MM(ps[bb][:, :n], wb[:, k, :], xn3[:, k, s0:s0 + n], k == 0, k == 7,
                               r=[wbk, ("xn", k, g)], w=[psk(bb)])
                        akey = ("AR", ai, g)
                        COPY("act", A[:, c0:c0 + n], ps[ba][:, :n], w=[psk(ba), akey])
                        tu = nxt("tmp", T6)
                        ACT(tmp[tu][:, :n], ps[ba][:, :n], AF.Identity, r=[vwk, "VT1"], w=[psk(ba), ("tmp", tu)],
                            scale=w2, bias=cbs)
                        STT("dve", tmp[tu][:, :n], A[:, c0 - 1:c0 - 1 + n], w1, tmp[tu][:, :n], ALU.mult, ALU.add,
                            r=[akey, prevkey, vwk], w=[("tmp", tu)])
                        STT("dve", tmp[tu][:, :n], A[:, c0 - 2:c0 - 2 + n], w0, tmp[tu][:, :n], ALU.mult, ALU.add,
                            r=[akey, prevkey, vwk], w=[("tmp", tu)])
                        tsl = nxt("tmp", T6)
                        if pend_gate[0] is not None:
                            pend_gate[0]()

                        def gate(jj=jj, s0=s0, n=n, bb=bb, tu=tu, tsl=tsl, g=g):
                            ACT(tmp[tsl][:, :n], tmp[tu][:, :n], AF.Silu, r=[("tmp", tu)], w=[("tmp", tsl)])
                            TTo("dve", gated3[:, jj, s0:s0 + n], ps[bb][:, :n], tmp[tsl][:, :n], ALU.mult,
                                r=[("tmp", tsl)], w=[psk(bb), ("gated", jj, g)])
                        pend_gate[0] = gate
                        prevkey = akey
                if pend_gate[0] is not None:
                    pend_gate[0]()
                    pend_gate[0] = None
                j0 = blk[0]
                P.mark(f"F{l}.out{bi}")
                for m in range(8):
                    wo, wok = LOADW(kview(nj, 128), w_out[l, j0 * 128:(j0 + nj) * 128, m * 128:(m + 1) * 128]
                                    .rearrange("(j p) c -> p j c", p=128))
                    for g, (s0, n) in enumerate(GR):
                        b = nxt("pW4", [4, 5, 6, 7])
                        for jj in range(nj):
                            MM(ps[b][:, :n], wo[:, jj, :], gated3[:, jj, s0:s0 + n], jj == 0, jj == nj - 1,
                               r=[wok, ("gated", jj, g)], w=[psk(b)])
                        TTo("dve", hT3[:, m, s0:s0 + n], ps[b][:, :n], hT3[:, m, s0:s0 + n], ALU.add,
                            w=[psk(b), ("hT", m, g)])

        def final():
            P.mark("final")
            ost = [arF[:, i * 1024:(i + 1) * 1024] for i in range(2)]
            for tt in range(16):
                s0 = tt * 128
                g = tt // 4
                b0, b1 = nxt("pF4", [(0, 1), (2, 3), (4, 5), (6, 7)])
                for k in range(8):
                    bk = b0 if k < 4 else b1
                    P.op("pe", lambda e, k=k, bk=bk, s0=s0: e.transpose(ps[bk][:, (k % 4) * 128:(k % 4 + 1) * 128],
                                                                         hT3[:, k, s0:s0 + 128], ident[:, :]),
                         r=[("hT", k, g), "ident"], w=[psk(bk)])
                ssc = nxt("ss", [0, 1]) * 4
                for hb, bk in enumerate((b0, b1)):
                    tq = nxt("tmp", T6)
                    ACT(tmp[tq][:, :], ps[bk][:, :], AF.Square, w=[psk(bk), ("tmp", tq)])
                    P.op("dve", lambda e, tq=tq, c=ssc + hb: e.reduce_sum(out=lamb[:, 264 + c:265 + c], in_=tmp[tq][:, :], axis=AX.X),
                         r=[("tmp", tq)], w=[("ss", ssc + hb)])
                TTo("dve", lamb[:, 264 + ssc + 2:264 + ssc + 3], lamb[:, 264 + ssc:264 + ssc + 1], lamb[:, 264 + ssc + 1:264 + ssc + 2],
                    ALU.add, r=[("ss", ssc), ("ss", ssc + 1)], w=[("ss", ssc + 2)])
                ACT(lamb[:, 264 + ssc + 3:264 + ssc + 4], lamb[:, 264 + ssc + 2:264 + ssc + 3], AF.Sqrt, r=[("ss", ssc + 2), "cst"],
                    w=[("ss", ssc + 3)], scale=1.0 / D, bias=epsc)
                RECIP(lamb[:, 264 + ssc + 3:264 + ssc + 4], lamb[:, 264 + ssc + 3:264 + ssc + 4], w=[("ss", ssc + 3)])
                oi = tt % 2
                for hb, bk in enumerate((b0, b1)):
                    STT("dve", ost[oi][:, hb * 512:(hb + 1) * 512], ps[bk][:, :], lamb[:, 264 + ssc + 3:264 + ssc + 4],
                        gfin[:, hb * 512:(hb + 1) * 512], ALU.mult, ALU.mult, r=[("ss", ssc + 3), "gfin"],
                        w=[psk(bk), ("ost", oi)])
                DMA("sp", out[s0:s0 + 128, :], ost[oi], gout[oi], r=[("ost", oi)], w=[("out", tt)])
            P.op("sp", lambda e: e.nop(), r=[("out", tt) for tt in range(16)])

        if do0:
            layer_A()
            P.barrier()
            ffn(0)
            P.barrier()
        if mode == "L0":
            DMA("sp", hmid[:, :], hT[:, :], gmisc, r=ALLHT, w=["hmid"])
            P.op("sp", lambda e: e.nop(), r=["hmid"])
        if do1:
            layer_B()
            P.barrier()
            ffn(1)
            P.barrier()
            final()
        P.mark("end")
        if PRINT_MARKS:
            print("MARKS", P.marks)
        P.finalize(nc, st)
        P.run_block(nc)
    return nc


def _rot_tables():
    pos = np.concatenate([np.arange(NMETA, NMETA + SEQ), np.arange(NMETA)]).astype(np.float32)
    f32 = np.float32

    def inv(theta, half):
        return (f32(1.0) / np.power(f32(theta), np.arange(half, dtype=np.float32) / f32(half))).astype(np.float32)

    tabA = np.zeros((2, 128, T), np.float32); tabA[0] = 1.0
    ia = inv(500000.0, 8)
    for r in range(128):
        d = r % 64
        if d < 16:
            ang = (pos * ia[d % 8]).astype(np.float32)
            tabA[0, r] = np.cos(ang)
            tabA[1, r] = -np.sin(ang) if d < 8 else np.sin(ang)
    tabB = np.zeros((2, 128, T), np.float32); tabB[0] = 1.0
    ib = inv(10000.0, 16)
    for r in range(64, 96):
        j = r - 64
        ang = (pos * ib[j % 16]).astype(np.float32)
        tabB[0, r] = np.cos(ang)
        tabB[1, r] = -np.sin(ang) if j < 16 else np.sin(ang)
    return tabA, tabB


def _prep(inputs):
    f = lambda a: np.ascontiguousarray(np.asarray(a, dtype=np.float32))
    I = {k: f(v) for k, v in inputs.items()}
    tabA, tabB = _rot_tables()
    w_qkv = I["a_w_qkv"][0]
    idx = np.arange(2048)
    d = idx % 64
    idx_sw = np.where(d < 8, idx + 8, np.where(d < 16, idx - 8, idx))
    w_qk_sw = np.ascontiguousarray(w_qkv[:, idx_sw])
    w_uq = I["b_w_uq"][0]
    idx = np.arange(1536)
    j = idx % 96
    idx_sw = np.where((j >= 64) & (j < 80), idx + 16, np.where(j >= 80, idx - 16, idx))
    w_uq_sw = np.ascontiguousarray(w_uq[:, idx_sw])
    w_dkv = I["kv_w_dkv"]
    w_dkv_r = np.zeros((D, 96), np.float32); w_dkv_r[:, 64:96] = w_dkv[:, 256:288]
    w_dkv_rs = np.zeros((D, 96), np.float32); w_dkv_rs[:, 64:96] = w_dkv[:, 256 + (np.arange(32) + 16) % 32]
    vecs1 = np.concatenate([
        I["a_attn_norm"][0].reshape(8, 128), I["ffn_norm"][0].reshape(8, 128), I["kv_norm"].reshape(8, 128),
        I["b_attn_norm"][0].reshape(8, 128), I["ffn_norm"][1].reshape(8, 128), I["kv_a_norm"].reshape(2, 128),
        I["b_q_a_norm"][0].reshape(3, 128), I["a_sub_norm"][0].reshape(1, 128),
        I["ffn_conv_b"][0].reshape(22, 128), I["ffn_conv_b"][1].reshape(22, 128)], axis=0)
    vecs2 = I["ffn_conv_w"][0].reshape(66, 128)
    vecs3 = I["ffn_conv_w"][1].reshape(66, 128)
    lam4 = np.stack([I["a_lambda_q1"][0], I["a_lambda_k1"][0], I["a_lambda_q2"][0], I["a_lambda_k2"][0]], axis=0)
    shared0 = dict(meta=I["meta_tokens"], w_qkv=w_qkv, w_qk_sw=w_qk_sw, a_w_o=I["a_w_o"][0], lam4=f(lam4), tabA=tabA)
    shared1 = dict(w_dkv=w_dkv, w_dkv_r=w_dkv_r, w_dkv_rs=w_dkv_rs, w_ukv=I["kv_w_ukv"], w_dq=I["b_w_dq"][0],
                   w_uq=w_uq, w_uq_sw=w_uq_sw, b_w_o=I["b_w_o"][0], tabB=tabB, final_norm=I["final_norm"])
    sharedf = dict(w_in=I["ffn_w_in"], w_out=I["ffn_w_out"], vecs1=f(vecs1), vecs2=f(vecs2), vecs3=f(vecs3))
    return I, shared0, shared1, sharedf


FUSED = True
_NC_CACHE = {}


def _get_nc(mode):
    if mode not in _NC_CACHE:
        _NC_CACHE[mode] = build_nc(mode)
    return _NC_CACHE[mode]


def kernel(**inputs):
    I, s0, s1, sf = _prep(inputs)
    B = I["x"].shape[0]
    cores = list(range(B))
    if FUSED:
        nc = _get_nc("ALL")
        maps = [dict(x=I["x"][b], **s0, **s1, **sf) for b in range(B)]
        res = run_bass_kernel_spmd(nc, maps, core_ids=cores)
        return np.stack([np.asarray(r["out"], dtype=np.float32) for r in res.results], axis=0)
    nc0 = _get_nc("L0")
    maps = [dict(x=I["x"][b], **s0, **sf) for b in range(B)]
    res0 = run_bass_kernel_spmd(nc0, maps, core_ids=cores)
    nc1 = _get_nc("L1")
    maps = [dict(hmid=np.asarray(res0.results[b]["hmid"], dtype=np.float32), **s1, **sf) for b in range(B)]
    res1 = run_bass_kernel_spmd(nc1, maps, core_ids=cores)
    return np.stack([np.asarray(r["out"], dtype=np.float32) for r in res1.results], axis=0)
```

```python
import contextlib
import numpy as np
import concourse.bass as bass
import concourse.mybir as mybir
from concourse.bass_utils import run_bass_kernel_spmd

F32 = mybir.dt.float32
BF16 = mybir.dt.bfloat16
AF = mybir.ActivationFunctionType
ALU = mybir.AluOpType
AX = mybir.AxisListType

ENGS = ["pe", "act", "dve", "pool", "sp"]
SEM_CH = 4000


class DmaGroup:
    def __init__(self, gid):
        self.gid = gid
        self.count = 0
        self.sem = None


class Prog:
    def __init__(self):
        self.ops = {e: [] for e in ENGS}
        self.last_w = {}
        self.readers = {}
        self.waited = {c: {e: -1 for e in ENGS} for c in ENGS}
        self.waited_dma = {c: {} for c in ENGS}
        self.groups = []
        self.pending = {e: [] for e in ENGS}
        self.marks = []

    def dma_group(self):
        g = DmaGroup(len(self.groups))
        self.groups.append(g)
        return g

    def op(self, eng, fn, r=(), w=(), dma=None):
        idx = len(self.ops[eng])
        o = dict(eng=eng, fn=fn, waits=[], idx=idx, sig=False, dma=None)
        deps = []
        for k in r:
            t = self.last_w.get(k)
            if t is not None:
                deps.append(t)
        for k in w:
            t = self.last_w.get(k)
            if t is not None:
                deps.append(t)
            deps.extend(self.readers.get(k, ()))
        deps.extend(self.pending[eng])
        self.pending[eng] = []
        best_e, best_d = {}, {}
        for t in deps:
            if t[0] == "e":
                if t[2] > best_e.get(t[1], -1):
                    best_e[t[1]] = t[2]
            else:
                if t[2] > best_d.get(t[1].gid, (None, 0))[1]:
                    best_d[t[1].gid] = (t[1], t[2])
        for e, i in best_e.items():
            if e == eng:
                if eng in ("pe", "sp"):
                    continue
            if i <= self.waited[eng][e]:
                continue
            self.waited[eng][e] = i
            self.ops[e][i]["sig"] = True
            o["waits"].append(("e", e, i))
        for gid, (g, v) in best_d.items():
            if v <= self.waited_dma[eng].get(gid, 0):
                continue
            self.waited_dma[eng][gid] = v
            o["waits"].append(("d", g, v))
        if dma is not None:
            dma.count += 16
            o["dma"] = dma
            tok = ("d", dma, dma.count)
        else:
            tok = ("e", eng, idx)
        for k in w:
            self.last_w[k] = tok
            self.readers[k] = []
        for k in r:
            if k in w:
                continue
            self.readers.setdefault(k, []).append(tok)
        self.ops[eng].append(o)
        return tok

    def mark(self, name):
        self.marks.append(name)
        self.op("pe", "MARK")

    def barrier(self):
        toks = []
        for e in ENGS:
            if self.ops[e]:
                for o in reversed(self.ops[e]):
                    if o["dma"] is None and o["fn"] != "MARK":
                        toks.append(("e", e, o["idx"]))
                        break
        for g in self.groups:
            if g.count:
                toks.append(("d", g, g.count))
        for e in ENGS:
            self.pending[e] = list(toks)

    def finalize(self, nc, stack):
        self.nsig = {}
        for e in ENGS:
            c = 0
            for o in self.ops[e]:
                if o["sig"]:
                    c += 1
                    o["sigval"] = c
            self.nsig[e] = c
        self.esems = {}
        for e in ENGS:
            n = (self.nsig[e] + SEM_CH - 1) // SEM_CH
            self.esems[e] = [stack.enter_context(nc.semaphore(f"s_{e}{i}")) for i in range(n)]
        for g in self.groups:
            if g.count:
                g.sem = stack.enter_context(nc.semaphore(f"d_{g.gid}"))
        self.mark_sem = stack.enter_context(nc.semaphore("phase_mark")) if self.marks else None

    def emit_engine(self, ename, eng):
        for o in self.ops[ename]:
            for t in o["waits"]:
                if t[0] == "e":
                    _, e, i = t
                    sv = self.ops[e][i]["sigval"]
                    sem = self.esems[e][(sv - 1) // SEM_CH]
                    val = (sv - 1) % SEM_CH + 1
                    eng.wait_ge(sem, val)
                else:
                    _, g, v = t
                    eng.wait_ge(g.sem, v)
            if o["fn"] == "MARK":
                eng.nop().then_inc(self.mark_sem, 1)
                continue
            ins = o["fn"](eng)
            if o["dma"] is not None:
                ins.then_inc(o["dma"].sem, 16)
            elif o["sig"]:
                sv = o["sigval"]
                ins.then_inc(self.esems[ename][(sv - 1) // SEM_CH], 1)

    def run_block(self, nc):
        with nc.Block() as block:
            @block.tensor
            def _(e):
                self.emit_engine("pe", e)

            @block.scalar
            def _(e):
                self.emit_engine("act", e)

            @block.vector
            def _(e):
                self.emit_engine("dve", e)

            @block.gpsimd
            def _(e):
                self.emit_engine("pool", e)

            @block.sync
            def _(e):
                self.emit_engine("sp", e)


D = 1024
SEQ = 2048
NMETA = 16
T = SEQ + NMETA
EPS = 1e-5
GR = [(0, 512), (512, 512), (1024, 512), (1536, 512), (2048, 16)]
TT = [(i * 128, 128) for i in range(16)] + [(2048, 16)]
DFF = 2816
FBLOCKS = [list(range(0, 8)), list(range(8, 15)), list(range(15, 22))]
LAM_INIT0 = 0.8 - 0.6 * 1.0
T6 = [0, 1, 2, 3, 4, 5]
T4 = [0, 1, 2, 3]
PRINT_MARKS = False


def build_nc(mode):
    nc = bass.Bass("TRN2", target_bir_lowering=False)
    dt_in = lambda n, s: nc.dram_tensor(n, s, F32, kind="ExternalInput").ap()
    dt_out = lambda n, s: nc.dram_tensor(n, s, F32, kind="ExternalOutput").ap()
    do0 = mode in ("ALL", "L0")
    do1 = mode in ("ALL", "L1")
    if do0:
        x = dt_in("x", [SEQ, D]); meta = dt_in("meta", [NMETA, D])
        w_qkv = dt_in("w_qkv", [D, 3072]); w_qk_sw = dt_in("w_qk_sw", [D, 2048])
        a_w_o = dt_in("a_w_o", [D, D]); lam4 = dt_in("lam4", [4, 64]); tabA = dt_in("tabA", [2, 128, T])
    if do1:
        w_dkv = dt_in("w_dkv", [D, 288]); w_dkv_r = dt_in("w_dkv_r", [D, 96]); w_dkv_rs = dt_in("w_dkv_rs", [D, 96])
        w_ukv = dt_in("w_ukv", [256, 2048]); w_dq = dt_in("w_dq", [D, 384])
        w_uq = dt_in("w_uq", [384, 1536]); w_uq_sw = dt_in("w_uq_sw", [384, 1536])
        b_w_o = dt_in("b_w_o", [D, D]); tabB = dt_in("tabB", [2, 128, T]); fin = dt_in("final_norm", [D])
    w_in = dt_in("w_in", [2, D, 2 * DFF]); w_out = dt_in("w_out", [2, DFF, D])
    vecs1 = dt_in("vecs1", [90, 128]); vecs2 = dt_in("vecs2", [66, 128]); vecs3 = dt_in("vecs3", [66, 128])
    if mode == "L0":
        hmid = dt_out("hmid", [128, 8 * T])
    if mode == "L1":
        hmid = dt_in("hmid", [128, 8 * T])
    if do1:
        out = dt_out("out", [SEQ, D])

    P = Prog()
    with contextlib.ExitStack() as st:
        sb = lambda n, s, d: st.enter_context(nc.sbuf_tensor(n, s, d))
        hT = sb("hT", [128, 8 * T], F32)
        xn = sb("xn", [128, 8 * T], BF16)
        arF = sb("arF", [128, 2 * (T + 2)], F32)
        arB = sb("arB", [128, 16736], BF16)
        tmp = [sb(f"tmp{i}", [128, 512], F32) for i in range(6)]
        wp = [sb(f"wp{i}", [128, 1024], BF16) for i in range(8)]
        ckvn = sb("ckvn", [128, 2 * T], BF16)
        pt = [sb(f"pt{i}", [128, 512], BF16) for i in range(4)]
        sq = [sb(f"sq{i}", [128, 512], BF16) for i in range(4)]
        rs = [sb(f"rs{i}", [128, 512], F32) for i in range(2)]
        sqe = [sb(f"sqe{i}", [128, 512], BF16) for i in range(2)]
        ocm = [sb(f"ocm{i}", [128, 512], F32) for i in range(2)]
        gfin = sb("gfin", [128, 1024], F32)
        ident = sb("ident", [128, 128], F32)
        ones = sb("ones", [128, 128], BF16)
        VT1 = sb("VT1", [128, 90], F32)
        VT2 = sb("VT2", [128, 66], F32)
        VT3 = sb("VT3", [128, 66], F32)
        cst = sb("cst", [128, 8], F32)
        lamb = sb("lamb", [128, 272], F32)
        ps = [st.enter_context(nc.psum_tensor(f"ps{i}", [128, 512], F32)) for i in range(8)]

        hT3 = hT[:, :].rearrange("p (k t) -> p k t", k=8)
        xn3 = xn[:, :].rearrange("p (k t) -> p k t", k=8)
        ckvn3 = ckvn[:, :].rearrange("p (k t) -> p k t", k=2)
        TC = arF[:, 0:T]
        TS_ = arF[:, T + 2:2 * T + 2]
        AR = [arF[:, 0:T + 2], arF[:, T + 2:2 * T + 4]]
        QB = [arB[:, 0:T], arB[:, T:2 * T]]
        KB = [arB[:, 2 * T:3 * T], arB[:, 3 * T:4 * T]]
        VB = [arB[:, 4 * T + i * 2176:4 * T + (i + 1) * 2176].rearrange("p (t c) -> p t c", c=128) for i in range(2)]
        oTb = [arB[:, 4 * T + 4352 + i * T:4 * T + 4352 + (i + 1) * T] for i in range(2)]
        gated3 = arB[:, 0:8 * T].rearrange("p (j t) -> p j t", j=8)
        epsc = cst[:, 0:1]

        def psk(b):
            return ("ps", b)

        def MM(o, lhsT, rhs, start, stop, r=(), w=()):
            P.op("pe", lambda e: e.matmul(o, lhsT, rhs, start=start, stop=stop), r=r, w=w)

        def ACT(o, i, func, r=(), w=(), scale=1.0, bias=None):
            if bias is None:
                P.op("act", lambda e: e.activation(out=o, in_=i, func=func, scale=scale), r=r, w=w)
            else:
                P.op("act", lambda e: e.activation(out=o, in_=i, func=func, scale=scale, bias=bias), r=r, w=w)

        def TTo(eng, o, a, b, op, r=(), w=()):
            P.op(eng, lambda e: e.tensor_tensor(out=o, in0=a, in1=b, op=op), r=r, w=w)

        def STT(eng, o, a, s, b, op0, op1, r=(), w=()):
            P.op(eng, lambda e: e.scalar_tensor_tensor(out=o, in0=a, scalar=s, in1=b, op0=op0, op1=op1), r=r, w=w)

        def TS(eng, o, a, s1, s2, op0, op1, r=(), w=()):
            P.op(eng, lambda e: e.tensor_scalar(out=o, in0=a, scalar1=s1, scalar2=s2, op0=op0, op1=op1), r=r, w=w)

        def RECIP(o, i, r=(), w=()):
            nf = o.shape[-1]
            if nf < 64:
                P.op("dve", lambda e: e.reciprocal(out=o, in_=i), r=r, w=w)
                return
            ACT(o, i, AF.Ln, r=r, w=w)
            ACT(o, o, AF.Exp, w=[k for k in w if k[0] != "ps"], scale=-1.0)

        def RSTD(o, i, scale, r=(), w=()):
            ACT(o, i, AF.Ln, r=list(r) + ["cst"], w=w, scale=scale, bias=epsc)
            ACT(o, o, AF.Exp, w=[k for k in w if k[0] != "ps"], scale=-0.5)

        def COPY(eng, o, i, r=(), w=()):
            if eng == "act":
                P.op("act", lambda e: e.activation(out=o, in_=i, func=AF.Copy), r=r, w=w)
            else:
                P.op(eng, lambda e: e.tensor_copy(out=o, in_=i), r=r, w=w)

        def MEMSET(eng, o, val, r=(), w=()):
            P.op(eng, lambda e: e.memset(o, val), r=r, w=w)

        def DMA(q, o, i, grp, r=(), w=()):
            P.op(q, lambda e: e.dma_start(out=o, in_=i), r=r, w=w, dma=grp)

        wgrp = [P.dma_group() for _ in range(8)]
        wctr = [0]

        def LOADW(view_fn, src):
            i = wctr[0] % 8
            wctr[0] += 1
            v = view_fn(wp[i])
            DMA("pool", v, src, wgrp[i], w=[("w", i)])
            return v, ("w", i)

        def kview(kt, m):
            return lambda buf: buf[:, 0:kt * m].rearrange("p (k m) -> p k m", k=kt)

        def ksrc(ap2d):
            return ap2d.rearrange("(k p) m -> p k m", p=128)

        rot = {}

        def nxt(name, lst):
            i = rot.get(name, 0)
            rot[name] = i + 1
            return lst[i % len(lst)]

        gmisc = P.dma_group()
        gtab = P.dma_group()
        gx = [P.dma_group() for _ in range(4)]
        gout = [P.dma_group() for _ in range(2)]

        MEMSET("pool", ident[:, :], 0.0, w=["ident"])
        P.op("pool", lambda e: e.affine_select(out=ident[:, :], in_=ident[:, :], pattern=[[-1, 128]],
                                                compare_op=ALU.not_equal, fill=1.0, base=0, channel_multiplier=1),
             r=["ident"], w=["ident"])
        MEMSET("pool", ones[:, :], 1.0, w=["ones"])
        MEMSET("pool", cst[:, 0:1], EPS, w=["cst"])
        MEMSET("pool", cst[0:64, 1:2], 0.0, w=["cst"])
        MEMSET("pool", cst[64:128, 1:2], -30000.0, w=["cst"])
        vsrc = ((vecs1, 90, VT1, "VT1"), (vecs2, 66, VT2, "VT2"), (vecs3, 66, VT3, "VT3"))
        gv = [P.dma_group() for _ in range(3)]
        for vi, (src, R, dst, nm) in enumerate(vsrc):
            DMA("sp", tmp[vi][0:R, 0:128], src[:, :], gv[vi], w=[("tmp", vi)])
        for vi, (src, R, dst, nm) in enumerate(vsrc):
            stg = tmp[vi][0:R, 0:128]
            P.op("pe", lambda e, stg=stg, R=R, vi=vi: e.transpose(ps[5 + vi][:, 0:R], stg, ident[0:R, 0:R]),
                 r=[("tmp", vi), "ident"], w=[psk(5 + vi)])
            COPY("dve", dst[:, :], ps[5 + vi][:, 0:R], w=[psk(5 + vi), nm])
        VG = {"a_attn": 0, "ffn0": 8, "kv": 16, "b_attn": 24, "ffn1": 32, "kv_a": 40, "q_a": 42, "sub": 45,
              "cb0": 46, "cb1": 68}
        ALLHT = [("hT", k, g) for k in range(8) for g in range(5)]

        if do0:
            xs = [arF[:, i * 1024:(i + 1) * 1024] for i in range(4)]
            for g in range(4):
                for t4 in range(4):
                    tt = g * 4 + t4
                    DMA("sp", xs[t4], x[tt * 128:(tt + 1) * 128, :], gx[t4], w=[("xs", t4)])
                    for k in range(8):
                        P.op("pe", lambda e, k=k, t4=t4: e.transpose(ps[k][:, t4 * 128:(t4 + 1) * 128],
                                                                     xs[t4][:, k * 128:(k + 1) * 128], ident[:, :]),
                             r=[("xs", t4), "ident"], w=[psk(k)])
                for k in range(8):
                    COPY("act" if k % 2 else "dve", hT3[:, k, g * 512:(g + 1) * 512], ps[k][:, :],
                         w=[psk(k), ("hT", k, g)])
            DMA("sp", xs[0][0:16, :], meta[:, :], gx[0], w=[("xs", 0)])
            for k in range(8):
                P.op("pe", lambda e, k=k: e.transpose(ps[k][:, 0:16], xs[0][0:16, k * 128:(k + 1) * 128], ident[0:16, 0:16]),
                     r=[("xs", 0), "ident"], w=[psk(k)])
                COPY("act" if k % 2 else "dve", hT3[:, k, 2048:2064], ps[k][:, 0:16], w=[psk(k), ("hT", k, 4)])
        else:
            DMA("sp", hT[:, :], hmid[:, :], gmisc, w=ALLHT)
        P.barrier()

        def norm_phase(gcol, order=(0, 1, 2, 3, 4)):
            P.mark(f"norm{gcol}")
            for g in order:
                s0, n = GR[g]
                nb = nxt("pN", [6, 7])
                for k in range(8):
                    sqi = nxt("sq", [0, 1, 2, 3])
                    ACT(sq[sqi][:, :n], hT3[:, k, s0:s0 + n], AF.Square, r=[("hT", k, g)], w=[("sq", sqi)])
                    MM(ps[nb][:, :n], ones[:, :], sq[sqi][:, :n], k == 0, k == 7, r=["ones", ("sq", sqi)], w=[psk(nb)])
                ri = nxt("rs", [0, 1])
                RSTD(rs[ri][:, :n], ps[nb][:, :n], 1.0 / D, w=[psk(nb), ("rs", ri)])
                for k in range(8):
                    STT("dve", xn3[:, k, s0:s0 + n], hT3[:, k, s0:s0 + n], VT1[:, gcol + k:gcol + k + 1], rs[ri][:, :n],
                        ALU.mult, ALU.mult, r=[("hT", k, g), ("rs", ri), "VT1"], w=[("xn", k, g)])

        def _pv(pd, ob, sbk, n, nk, Vv, vkey, extra_r=()):
            pti, kt, kn, qlo, idx = pd
            MM(ps[ob][:, qlo:n], Vv[:kn, kt, :], ptl[pti][:kn, qlo:n], idx == 0, idx == nk - 1,
               r=[("pt", pti), vkey] + list(extra_r), w=[psk(ob)])
            if sbk is not None:
                MM(ps[sbk][:, qlo:n], ones[:kn, :], ptl[pti][:kn, qlo:n], idx == 0, idx == nk - 1,
                   r=[("pt", pti), "ones"], w=[psk(sbk)])

        CHUNK = 2
        chunk_cfg = [2]
        ptl = [pt[0], pt[1], pt[2], pt[3]]
        SBANKS = [0, 1, 6, 7]
        obanks = [[2, 3, 4, 5]]

        def attention_unit(maps, scale, evac_map, combine, sep_sums, fillers=(), carry=None, last=True):
            fillers = list(fillers)
            deferred = list(carry) if carry else []
            chain = [0]

            def run_deferred(item):
                d = item[1]()
                if d is not None:
                    deferred.insert(0, [item[0], d])

            def pop_filler():
                f = fillers.pop(0)
                if getattr(f, "needs_flush", False):
                    while deferred and deferred[0][0] == 0:
                        run_deferred(deferred.pop(0))
                f()

            CH = chunk_cfg[0]
            steps_left = [len(maps) * sum(-(-(5 + 4 * g) // CH) - 1 for g in range(4))]
            for g in [4, 0, 1, 2, 3]:
                s0, n = GR[g]
                kts = [16] if g == 4 else [16] + list(range(0, 4 * g + 4))
                nk = len(kts)
                chunks = [list(range(i, min(i + CH, nk))) for i in range(0, nk, CH)]
                for mi, (Kb, Qb, r0, nr, kkeys, qkey, Vv, vkey) in enumerate(maps):
                    chain[0] += 1
                    while deferred and deferred[0][0] < chain[0] - 1:
                        run_deferred(deferred.pop(0))
                    ob = nxt("pO", obanks[0])
                    sbk = nxt("pO", obanks[0]) if sep_sums else None

                    def emit_pv(chunk, ob=ob, sbk=sbk, n=n, nk=nk, Vv=Vv, vkey=vkey):
                        keys = [("pt", c[0]) for c in chunk] + [("ptm", c[0]) for c in chunk]
                        for ci, c in enumerate(chunk):
                            _pv(c, ob, sbk, n, nk, Vv, vkey, extra_r=keys if ci == 0 else ())

                    prev = None
                    for ch in chunks:
                        cur = []
                        for idx in ch:
                            kt = kts[idx]
                            ks0, kn = TT[kt]
                            qlo = 0
                            diag = g < 4 and kt != 16 and kt >= 4 * g
                            if diag:
                                qlo = 128 * (kt - 4 * g)
                            sbank = nxt("pS", SBANKS)
                            pti = nxt("pt", list(range(len(ptl))))
                            MM(ps[sbank][:kn, qlo:n], Kb[r0:r0 + nr, ks0:ks0 + kn], Qb[r0:r0 + nr, s0 + qlo:s0 + n], True, True,
                               r=list(kkeys) + [qkey(g)], w=[psk(sbank)])
                            if diag:
                                ACT(ptl[pti][:kn, qlo + 64:n], ps[sbank][:kn, qlo + 64:n], AF.Exp, r=[psk(sbank)],
                                    w=[("pt", pti)], scale=scale)
                                ACT(ptl[pti][:kn, qlo:qlo + 64], ps[sbank][:kn, qlo:qlo + 64], AF.Exp, r=[psk(sbank), "cst"],
                                    w=[("ptm", pti)], scale=scale, bias=cst[:kn, 1:2])
                            else:
                                ACT(ptl[pti][:kn, qlo:n], ps[sbank][:kn, qlo:n], AF.Exp, r=[psk(sbank)],
                                    w=[("pt", pti), ("ptm", pti)], scale=scale)
                            cur.append((pti, kt, kn, qlo, idx))
                        if prev is None:
                            for _ in range(min(2, len(fillers))):
                                pop_filler()
                        if prev is not None:
                            emit_pv(prev)
                            steps_left[0] -= 1
                            if deferred:
                                run_deferred(deferred.pop(0))
                            npop = -(-len(fillers) // max(steps_left[0] + 1, 1))
                            for _ in range(min(npop, len(fillers))):
                                pop_filler()
                        prev = cur
                    emit_pv(prev)
                    deferred.append([chain[0], (lambda mi=mi, g=g, s0=s0, n=n, ob=ob, sbk=sbk: evac_map(mi, g, s0, n, ob, sbk))])
                if combine is not None:
                    deferred.append([chain[0], (lambda g=g, s0=s0, n=n: combine(g, s0, n))])
            while fillers:
                pop_filler()
            if last:
                while deferred:
                    run_deferred(deferred.pop(0))
                return []
            return [[0, d[1]] for d in deferred]

        def wo_tiles(wo, wk, oT_slot, okeys):
            tiles = []
            for g in [4, 0, 1, 2, 3]:
                s0, n = GR[g]
                for m in range(8):
                    def f(m=m, g=g, s0=s0, n=n):
                        b = nxt("pS", SBANKS)
                        MM(ps[b][:, :n], wo[:, m * 128:(m + 1) * 128], oTb[oT_slot][:, s0:s0 + n], True, True,
                           r=[wk] + okeys(g), w=[psk(b)])
                        TTo("dve", hT3[:, m, s0:s0 + n], ps[b][:, :n], hT3[:, m, s0:s0 + n], ALU.add,
                            w=[psk(b), ("hT", m, g)])
                    f.needs_flush = True
                    tiles.append(f)
            return tiles

        def proj_rot(dst, wfn, wk, wsfn, wsk, ktiles, src3, srckey, rows, dkey, lazy=False):
            out = []
            for g, (s0, n) in enumerate(GR):
                def f(g=g, s0=s0, n=n):
                    if lazy:
                        b1 = nxt("pS", SBANKS)
                        b2 = nxt("pS", SBANKS)
                    else:
                        b1, b2 = nxt("pP", [(0, 1), (6, 7)])
                    for k in range(ktiles):
                        MM(ps[b1][0:rows, :n], wfn(k), src3[:, k, s0:s0 + n], k == 0, k == ktiles - 1,
                           r=[wk, (srckey, k, g)], w=[psk(b1)])
                    for k in range(ktiles):
                        MM(ps[b2][0:rows, :n], wsfn(k), src3[:, k, s0:s0 + n], k == 0, k == ktiles - 1,
                           r=[wsk, (srckey, k, g)], w=[psk(b2)])
                    t1 = nxt("tmp", T4)
                    t2 = nxt("tmp", T4)
                    TTo("dve", tmp[t1][0:rows, :n], ps[b1][0:rows, :n], TC[0:rows, s0:s0 + n], ALU.mult,
                        r=["tab"], w=[psk(b1), ("tmp", t1)])
                    TTo("dve", tmp[t2][0:rows, :n], ps[b2][0:rows, :n], TS_[0:rows, s0:s0 + n], ALU.mult,
                        r=["tab"], w=[psk(b2), ("tmp", t2)])
                    TTo("pool" if lazy else "dve", dst[0:rows, s0:s0 + n], tmp[t1][0:rows, :n], tmp[t2][0:rows, :n], ALU.add,
                        r=[("tmp", t1), ("tmp", t2)], w=[dkey(g)])
                if lazy:
                    out.append(f)
                else:
                    f()
            return out

        def interleave(a, b):
            out = []
            na, nb_ = len(a), len(b)
            ia = ib = 0
            while ia < na or ib < nb_:
                if ib >= nb_ or (ia < na and ia * nb_ <= ib * na):
                    out.append(a[ia]); ia += 1
                else:
                    out.append(b[ib]); ib += 1
            return out


        def layer_A():
            DMA("sp", TC, tabA[0], gtab, w=["tab"])
            DMA("sp", TS_, tabA[1], gtab, w=["tab"])
            for i in range(4):
                DMA("sp", lamb[:, i * 64:(i + 1) * 64], lam4[i].partition_broadcast(128), gmisc, w=["lamb"])
            def load_proj(h):
                c = h * 128
                W = {}
                for nm, c0 in (("Q", c), ("K", 1024 + c)):
                    W[nm] = LOADW(kview(8, 128), ksrc(w_qkv[:, c0:c0 + 128]))
                    W[nm + "s"] = LOADW(kview(8, 128), ksrc(w_qk_sw[:, c0:c0 + 128]))
                W["V"] = LOADW(kview(8, 128), ksrc(w_qkv[:, 2048 + c:2048 + c + 128]))
                return W

            W0 = load_proj(0)
            norm_phase(VG["a_attn"])
            TTo("dve", lamb[:, 0:64], lamb[:, 0:64], lamb[:, 64:128], ALU.mult, w=["lamb"])
            TTo("dve", lamb[:, 128:192], lamb[:, 128:192], lamb[:, 192:256], ALU.mult, w=["lamb"])
            P.op("dve", lambda e: e.reduce_sum(out=lamb[:, 256:257], in_=lamb[:, 0:64], axis=AX.X), w=["lamb"])
            P.op("dve", lambda e: e.reduce_sum(out=lamb[:, 257:258], in_=lamb[:, 128:192], axis=AX.X), w=["lamb"])
            ACT(lamb[:, 258:260], lamb[:, 256:258], AF.Exp, w=["lamb"])
            TTo("dve", lamb[:, 260:261], lamb[:, 258:259], lamb[:, 259:260], ALU.subtract, w=["lamb"])
            TS("dve", lamb[:, 261:262], lamb[:, 260:261], LAM_INIT0, -1.0, ALU.add, ALU.mult, w=["lamb"])
            TS("dve", lamb[:, 262:263], VT1[:, VG["sub"]:VG["sub"] + 1], 1.0 - LAM_INIT0, 0.0, ALU.mult, ALU.add,
               r=["VT1"], w=["lamb"])
            neglam = lamb[:, 261:262]
            gsub = lamb[:, 262:263]


            def proj_unit(h, W, lazy):
                slot = h % 2
                Qb, Kb, Vv = QB[slot], KB[slot], VB[slot]
                out = []
                for (dst, nm) in ((Qb, "Q"), (Kb, "K")):
                    (wv, wk), (wsv, wsk) = W[nm], W[nm + "s"]
                    out += proj_rot(dst, (lambda k, wv=wv: wv[:, k, :]), wk, (lambda k, wsv=wsv: wsv[:, k, :]), wsk,
                                    8, xn3, "xn", 128, (lambda g, nm=nm, slot=slot: (nm + "f", slot, g)), lazy=lazy)
                wvv, wvk = W["V"]
                vkey = ("Vf", slot)
                for t0 in range(0, 17, 4):
                    def fv(t0=t0):
                        vb = nxt("pS", SBANKS) if lazy else nxt("pO", [2, 3, 4, 5])
                        for tt in range(t0, min(t0 + 4, 17)):
                            s0, n = TT[tt]
                            c0 = (tt % 4) * 128
                            g = min(tt // 4, 4)
                            for k in range(8):
                                MM(ps[vb][:n, c0:c0 + 128], xn3[:, k, s0:s0 + n], wvv[:, k, :], k == 0, k == 7,
                                   r=[wvk, ("xn", k, g)], w=[psk(vb)])
                        if t0 < 16:
                            COPY("act", Vv[:, t0:t0 + 4, :], ps[vb][:, :].rearrange("p (t c) -> p t c", c=128),
                                 w=[psk(vb), vkey])
                        else:
                            COPY("act", Vv[0:16, 16, :], ps[vb][0:16, 0:128], w=[psk(vb), vkey])
                    if lazy:
                        out.append(fv)
                    else:
                        fv()
                return out

            P.mark("A0.proj")
            proj_unit(0, W0, False)
            prev_fill = []
            carryA = []
            for h in range(8):
                slot = h % 2
                Qb, Kb, Vv = QB[slot], KB[slot], VB[slot]
                vkey = ("Vf", slot)
                next_proj = []
                if h + 1 < 8:
                    next_proj = proj_unit(h + 1, load_proj(h + 1), True)
                wo, wok = LOADW(lambda b: b[:, :], a_w_o[h * 128:(h + 1) * 128, :])
                this_fill = wo_tiles(wo, wok, slot, (lambda g, slot=slot: [("oT", slot, g)]))
                prev_fill = interleave(prev_fill, next_proj)

                o_t = {}

                def evac_map(mi, g, s0, n, ob, sbk, o_t=o_t):
                    to = 4 + mi
                    RECIP(tmp[to][:, :n], ps[sbk][:, :n], w=[psk(sbk), ("tmp", to)])
                    TTo("dve", tmp[to][:, :n], ps[ob][:, :n], tmp[to][:, :n], ALU.mult, w=[psk(ob), ("tmp", to)])
                    o_t[mi] = to

                def combine(g, s0, n, slot=slot, o_t=o_t):
                    t1, t2 = o_t[0], o_t[1]
                    STT("dve", tmp[t1][:, :n], tmp[t2][:, :n], neglam, tmp[t1][:, :n], ALU.mult, ALU.add,
                        r=[("tmp", t2), "lamb"], w=[("tmp", t1)])
                    sqi = nxt("sqe", [0, 1])
                    ACT(sqe[sqi][:, :n], tmp[t1][:, :n], AF.Square, r=[("tmp", t1)], w=[("sqe", sqi)])
                    oi = nxt("ocm", [0, 1])
                    COPY("pool", ocm[oi][:, :n], tmp[t1][:, :n], r=[("tmp", t1)], w=[("ocm", oi)])

                    def tail(g=g, s0=s0, n=n, sqi=sqi, oi=oi, slot=slot):
                        nb = nxt("pS", SBANKS)
                        MM(ps[nb][:, :n], ones[:, :], sqe[sqi][:, :n], True, True, r=["ones", ("sqe", sqi)], w=[psk(nb)])
                        ri = nxt("rs", [0, 1])
                        RSTD(rs[ri][:, :n], ps[nb][:, :n], 1.0 / 128, w=[psk(nb), ("rs", ri)])
                        STT("dve", oTb[slot][:, s0:s0 + n], ocm[oi][:, :n], gsub, rs[ri][:, :n], ALU.mult, ALU.mult,
                            r=[("ocm", oi), ("rs", ri), "lamb"], w=[("oT", slot, g)])
                    return tail

                P.mark(f"A{h}.attn")
                kk = [("Kf", slot, g) for g in range(5)]
                qk = lambda g, slot=slot: ("Qf", slot, g)
                carryA = attention_unit([(Kb, Qb, 0, 64, kk, qk, Vv, vkey), (Kb, Qb, 64, 64, kk, qk, Vv, vkey)], 64 ** -0.5,
                                        evac_map, combine, True, fillers=prev_fill, carry=carryA, last=(h == 7))
                prev_fill = this_fill
            P.mark("A.wo_last")
            for f in prev_fill:
                f()

        def layer_B():
            DMA("sp", TC, tabB[0], gtab, w=["tab"])
            DMA("sp", TS_, tabB[1], gtab, w=["tab"])
            wl0, wl0k = LOADW(kview(8, 128), ksrc(w_dkv[:, 0:128]))
            wl1, wl1k = LOADW(kview(8, 128), ksrc(w_dkv[:, 128:256]))
            wr, wrk = LOADW(kview(8, 96), ksrc(w_dkv_r[:, :]))
            wrs, wrsk = LOADW(kview(8, 96), ksrc(w_dkv_rs[:, :]))
            DMA("sp", gfin[:, :], fin.partition_broadcast(128), gmisc, w=["gfin"])
            norm_phase(VG["kv"])
            P.mark("B.kv")
            for g, (s0, n) in enumerate(GR):
                cb = [nxt("pO", [2, 3, 4, 5]) for _ in range(2)]
                for j, (wl, wlk) in enumerate(((wl0, wl0k), (wl1, wl1k))):
                    for k in range(8):
                        MM(ps[cb[j]][:, :n], wl[:, k, :], xn3[:, k, s0:s0 + n], k == 0, k == 7,
                           r=[wlk, ("xn", k, g)], w=[psk(cb[j])])
                nb = nxt("pN", [6, 7])
                for j in range(2):
                    sqi = nxt("sq", [0, 1, 2, 3])
                    ACT(sq[sqi][:, :n], ps[cb[j]][:, :n], AF.Square, w=[psk(cb[j]), ("sq", sqi)])
                    MM(ps[nb][:, :n], ones[:, :], sq[sqi][:, :n], j == 0, j == 1, r=["ones", ("sq", sqi)], w=[psk(nb)])
                ri = nxt("rs", [0, 1])
                RSTD(rs[ri][:, :n], ps[nb][:, :n], 1.0 / 256, w=[psk(nb), ("rs", ri)])
                for j in range(2):
                    STT("dve", ckvn3[:, j, s0:s0 + n], ps[cb[j]][:, :n], VT1[:, VG["kv_a"] + j:VG["kv_a"] + j + 1],
                        rs[ri][:, :n], ALU.mult, ALU.mult, r=[("rs", ri), "VT1"], w=[psk(cb[j]), ("ckvn", j, g)])
                b1, b2 = nxt("pP", [(0, 1)])
                for k in range(8):
                    MM(ps[b1][0:96, :n], wr[:, k, :], xn3[:, k, s0:s0 + n], k == 0, k == 7, r=[wrk, ("xn", k, g)], w=[psk(b1)])
                for k in range(8):
                    MM(ps[b2][0:96, :n], wrs[:, k, :], xn3[:, k, s0:s0 + n], k == 0, k == 7, r=[wrsk, ("xn", k, g)], w=[psk(b2)])
                t1 = nxt("tmp", T6)
                t2 = nxt("tmp", T6)
                TTo("dve", tmp[t1][64:96, :n], ps[b1][64:96, :n], TC[64:96, s0:s0 + n], ALU.mult,
                    r=["tab"], w=[psk(b1), ("tmp", t1)])
                TTo("dve", tmp[t2][64:96, :n], ps[b2][64:96, :n], TS_[64:96, s0:s0 + n], ALU.mult,
                    r=["tab"], w=[psk(b2), ("tmp", t2)])
                for hh in range(2):
                    TTo("pool", KB[hh][64:96, s0:s0 + n], tmp[t1][64:96, :n], tmp[t2][64:96, :n], ALU.add,
                        r=[("tmp", t1), ("tmp", t2)], w=[("krope", hh, g)])
            wq = [LOADW(kview(8, 128), ksrc(w_dq[:, j * 128:(j + 1) * 128])) for j in range(3)]
            norm_phase(VG["b_attn"])
            P.mark("B.dq")
            for g, (s0, n) in enumerate(GR):
                cb = [nxt("pO", [2, 3, 4, 5]) for _ in range(3)]
                for j in range(3):
                    for k in range(8):
                        MM(ps[cb[j]][:, :n], wq[j][0][:, k, :], xn3[:, k, s0:s0 + n], k == 0, k == 7,
                           r=[wq[j][1], ("xn", k, g)], w=[psk(cb[j])])
                nb = nxt("pN", [6, 7])
                for j in range(3):
                    sqi = nxt("sq", [0, 1, 2, 3])
                    ACT(sq[sqi][:, :n], ps[cb[j]][:, :n], AF.Square, w=[psk(cb[j]), ("sq", sqi)])
                    MM(ps[nb][:, :n], ones[:, :], sq[sqi][:, :n], j == 0, j == 2, r=["ones", ("sq", sqi)], w=[psk(nb)])
                ri = nxt("rs", [0, 1])
                RSTD(rs[ri][:, :n], ps[nb][:, :n], 1.0 / 384, w=[psk(nb), ("rs", ri)])
                for j in range(3):
                    STT("dve", xn3[:, j, s0:s0 + n], ps[cb[j]][:, :n], VT1[:, VG["q_a"] + j:VG["q_a"] + j + 1],
                        rs[ri][:, :n], ALU.mult, ALU.mult, r=[("rs", ri), "VT1"],
                        w=[psk(cb[j])] + [("xn", kk_, g) for kk_ in range(8)])
            MEMSET("pool", VB[0][:, :, 64:128], 1.0, w=[("Vf", 0)])
            MEMSET("pool", VB[1][:, :, 0:64], 1.0, w=[("Vf", 1)])

            def load_proj_B(u):
                W = {}
                W["q"] = LOADW(kview(3, 192), ksrc(w_uq[:, u * 192:(u + 1) * 192]))
                W["qs"] = LOADW(kview(3, 192), ksrc(w_uq_sw[:, u * 192:(u + 1) * 192]))
                W["kv"] = LOADW(kview(2, 256), ksrc(w_ukv[:, u * 256:(u + 1) * 256]))
                return W

            P.barrier()
            Qsets = [[QB[0], QB[1]], [xn3[:, 3, :], xn3[:, 4, :]]]
            Ksets = [[KB[0], KB[1]], [xn3[:, 5, :], xn3[:, 6, :]]]
            for hh in range(2):
                COPY("dve" if hh else "act", Ksets[1][hh][64:96, :], KB[hh][64:96, :], r=[("krope", hh, g) for g in range(5)],
                     w=[("krope2", hh)])

            def proj_B(u, W, lazy):
                st_ = u % 2
                (wq2, wq2k), (wq2s, wq2sk), (wkv2, wkv2k) = W["q"], W["qs"], W["kv"]
                out = []
                for hh in range(2):
                    out += proj_rot(Qsets[st_][hh], (lambda k, hh=hh, wq2=wq2: wq2[:, k, hh * 96:(hh + 1) * 96]), wq2k,
                                    (lambda k, hh=hh, wq2s=wq2s: wq2s[:, k, hh * 96:(hh + 1) * 96]), wq2sk,
                                    3, xn3, "xn", 96, (lambda g, hh=hh, st_=st_: ("Qf", st_, hh, g)), lazy=lazy)
                    for g, (s0, n) in enumerate(GR):
                        def fk(hh=hh, g=g, s0=s0, n=n):
                            b = nxt("pS", SBANKS) if lazy else nxt("pO", [2, 3, 4, 5])
                            for k in range(2):
                                MM(ps[b][0:64, :n], wkv2[:, k, hh * 128:hh * 128 + 64], ckvn3[:, k, s0:s0 + n], k == 0, k == 1,
                                   r=[wkv2k, ("ckvn", k, g)], w=[psk(b)])
                            COPY("dve", Ksets[st_][hh][0:64, s0:s0 + n], ps[b][0:64, :n], w=[psk(b), ("Kf", st_, hh, g)])
                        if lazy:
                            out.append(fk)
                        else:
                            fk()
                return out

            def vproj_B(W):
                wkv2, wkv2k = W["kv"]
                wv2 = wkv2.rearrange("p k (h c) -> p k h c", h=2)
                for tt, (s0, n) in enumerate(TT):
                    if tt % 4 == 0:
                        vb = nxt("pS", SBANKS)
                    c0 = (tt % 4) * 128
                    g = min(tt // 4, 4)
                    for k in range(2):
                        MM(ps[vb][:n, c0:c0 + 128].rearrange("p (h c) -> p h c", h=2), ckvn3[:, k, s0:s0 + n],
                           wv2[:, k, :, 64:128], k == 0, k == 1, r=[wkv2k, ("ckvn", k, g)], w=[psk(vb)])
                    pv3 = ps[vb][:, :].rearrange("p (t c) -> p t c", c=128)
                    if tt % 4 == 3:
                        COPY("dve", VB[0][:, tt - 3:tt + 1, 0:64], pv3[:, :, 0:64], w=[psk(vb), ("Vf", 0)])
                        COPY("dve", VB[1][:, tt - 3:tt + 1, 64:128], pv3[:, :, 64:128], w=[psk(vb), ("Vf", 1)])
                    elif tt == 16:
                        COPY("act", VB[0][0:16, 16, 0:64], ps[vb][0:16, 0:64], w=[psk(vb), ("Vf", 0)])
                        COPY("act", VB[1][0:16, 16, 64:128], ps[vb][0:16, 64:128], w=[psk(vb), ("Vf", 1)])

            P.mark("B0.proj")
            Wcur = load_proj_B(0)
            proj_B(0, Wcur, False)
            obanks[0] = [2, 3]
            SBANKS.extend([4, 5])
            ptl.extend([sq[0], sq[1]])
            chunk_cfg[0] = 3
            prev_fill = []
            carryB = []
            for u in range(8):
                slot = u % 2
                P.mark(f"B{u}.vproj")
                vproj_B(Wcur)
                next_proj = []
                if u + 1 < 8:
                    Wcur = load_proj_B(u + 1)
                    next_proj = proj_B(u + 1, Wcur, True)
                wo, wok = LOADW(lambda b: b[:, :], b_w_o[u * 128:(u + 1) * 128, :])
                this_fill = wo_tiles(wo, wok, slot, (lambda g, slot=slot: [("oTh", slot, g, 0), ("oTh", slot, g, 1)]))
                prev_fill = interleave(prev_fill, next_proj)

                def evac_map(hh, g, s0, n, ob, sbk, slot=slot):
                    r0 = hh * 64
                    q0 = 64 - r0
                    tr = nxt("tmpe", [4, 5])
                    RECIP(tmp[tr][r0:r0 + 64, :n], ps[ob][q0:q0 + 64, :n], w=[psk(ob), ("tmp", tr)])
                    TTo("dve", oTb[slot][r0:r0 + 64, s0:s0 + n], ps[ob][r0:r0 + 64, :n], tmp[tr][r0:r0 + 64, :n],
                        ALU.mult, r=[("tmp", tr)], w=[psk(ob), ("oTh", slot, g, hh)])

                P.mark(f"B{u}.attn")
                maps = []
                for hh in range(2):
                    kk = [("Kf", slot, hh, g) for g in range(5)]
                    kk += [("krope", hh, g) for g in range(5)] if slot == 0 else [("krope2", hh)]
                    maps.append((Ksets[slot][hh], Qsets[slot][hh], 0, 96, kk,
                                 (lambda g, hh=hh, slot=slot: ("Qf", slot, hh, g)), VB[hh], ("Vf", hh)))
                carryB = attention_unit(maps, 96 ** -0.5, evac_map, None, False, fillers=prev_fill, carry=carryB, last=(u == 7))
                prev_fill = this_fill
            P.mark("B.wo_last")
            for f in prev_fill:
                f()

        def ffn(l):
            VW = VT2 if l == 0 else VT3
            vwk = "VT2" if l == 0 else "VT3"
            cbcol = VG["cb0"] if l == 0 else VG["cb1"]
            def load_tile(j):
                return (LOADW(kview(8, 128), ksrc(w_in[l, :, j * 128:(j + 1) * 128])),
                        LOADW(kview(8, 128), ksrc(w_in[l, :, DFF + j * 128:DFF + (j + 1) * 128])))

            pre_ahead = [load_tile(FBLOCKS[0][0]), load_tile(FBLOCKS[0][1])]
            norm_phase(VG["ffn0"] if l == 0 else VG["ffn1"], order=(4, 0, 1, 2, 3))
            for i in range(2):
                MEMSET("pool", AR[i][:, 0:2], 0.0, w=[("ARz", i)])
            order = [4, 0, 1, 2, 3]
            pend_gate = [None]
            for bi, blk in enumerate(FBLOCKS):
                P.mark(f"F{l}.in{bi}")
                nj = len(blk)
                if bi == 0:
                    ahead = pre_ahead
                else:
                    ahead = [load_tile(blk[0])] + ([load_tile(blk[1])] if nj > 1 else [])
                for jj, j in enumerate(blk):
                    (wa, wak), (wb, wbk) = ahead.pop(0)
                    if jj + 2 < nj:
                        ahead.append(load_tile(blk[jj + 2]))
                    ai = j % 2
                    A = AR[ai]
                    w0 = VW[:, 0 * 22 + j:0 * 22 + j + 1]
                    w1 = VW[:, 1 * 22 + j:1 * 22 + j + 1]
                    w2 = VW[:, 2 * 22 + j:2 * 22 + j + 1]
                    cbs = VT1[:, cbcol + j:cbcol + j + 1]
                    prevkey = ("ARz", ai)
                    for g in order:
                        s0, n = GR[g]
                        c0 = (0 if g == 4 else 16 + 512 * g) + 2
                        ba, bb = nxt("pF4", [(0, 1), (2, 3), (4, 5), (6, 7)])
                        for k in range(8):
                            MM(ps[ba][:, :n], wa[:, k, :], xn3[:, k, s0:s0 + n], k == 0, k == 7,
                               r=[wak, ("xn", k, g)] + ([wbk] if k == 0 else []), w=[psk(ba)] + ([psk(bb)] if k == 0 else []))
                        for k in range(8):
                            MM(ps[bb][:, :n], wb[:, k, :], xn3[:, k, s0:s0 + n], k == 0, k == 7,
                               r=[wbk, ("xn", k, g)], w=[psk(bb)])
                        akey = ("AR", ai, g)
                        COPY("act", A[:, c0:c0 + n], ps[ba][:, :n], w=[psk(ba), akey])
                        tu = nxt("tmp", T6)
                        ACT(tmp[tu][:, :n], ps[ba][:, :n], AF.Identity, r=[vwk, "VT1"], w=[psk(ba), ("tmp", tu)],
                            scale=w2, bias=cbs)
                        STT("dve", tmp[tu][:, :n], A[:, c0 - 1:c0 - 1 + n], w1, tmp[tu][:, :n], ALU.mult, ALU.add,
                            r=[akey, prevkey, vwk], w=[("tmp", tu)])
                        STT("dve", tmp[tu][:, :n], A[:, c0 - 2:c0 - 2 + n], w0, tmp[tu][:, :n], ALU.mult, ALU.add,
                            r=[akey, prevkey, vwk], w=[("tmp", tu)])
                        tsl = nxt("tmp", T6)
                        if pend_gate[0] is not None:
                            pend_gate[0]()

                        def gate(jj=jj, s0=s0, n=n, bb=bb, tu=tu, tsl=tsl, g=g):
                            ACT(tmp[tsl][:, :n], tmp[tu][:, :n], AF.Silu, r=[("tmp", tu)], w=[("tmp", tsl)])
                            TTo("dve", gated3[:, jj, s0:s0 + n], ps[bb][:, :n], tmp[tsl][:, :n], ALU.mult,
                                r=[("tmp", tsl)], w=[psk(bb), ("gated", jj, g)])
                        pend_gate[0] = gate
                        prevkey = akey
                if pend_gate[0] is not None:
                    pend_gate[0]()
                    pend_gate[0] = None
                j0 = blk[0]
                P.mark(f"F{l}.out{bi}")
                for m in range(8):
                    wo, wok = LOADW(kview(nj, 128), w_out[l, j0 * 128:(j0 + nj) * 128, m * 128:(m + 1) * 128]
                                    .rearrange("(j p) c -> p j c", p=128))
                    for g, (s0, n) in enumerate(GR):
                        b = nxt("pW4", [4, 5, 6, 7])
                        for jj in range(nj):
                            MM(ps[b][:, :n], wo[:, jj, :], gated3[:, jj, s0:s0 + n], jj == 0, jj == nj - 1,
                               r=[wok, ("gated", jj, g)], w=[psk(b)])
                        TTo("dve", hT3[:, m, s0:s0 + n], ps[b][:, :n], hT3[:, m, s0:s0 + n], ALU.add,
                            w=[psk(b), ("hT", m, g)])

        def final():
            P.mark("final")
            ost = [arF[:, i * 1024:(i + 1) * 1024] for i in range(2)]
            for tt in range(16):
                s0 = tt * 128
                g = tt // 4
                b0, b1 = nxt("pF4", [(0, 1), (2, 3), (4, 5), (6, 7)])
                for k in range(8):
                    bk = b0 if k < 4 else b1
                    P.op("pe", lambda e, k=k, bk=bk, s0=s0: e.transpose(ps[bk][:, (k % 4) * 128:(k % 4 + 1) * 128],
                                                                         hT3[:, k, s0:s0 + 128], ident[:, :]),
                         r=[("hT", k, g), "ident"], w=[psk(bk)])
                ssc = nxt("ss", [0, 1]) * 4
                for hb, bk in enumerate((b0, b1)):
                    tq = nxt("tmp", T6)
                    ACT(tmp[tq][:, :], ps[bk][:, :], AF.Square, w=[psk(bk), ("tmp", tq)])
                    P.op("dve", lambda e, tq=tq, c=ssc + hb: e.reduce_sum(out=lamb[:, 264 + c:265 + c], in_=tmp[tq][:, :], axis=AX.X),
                         r=[("tmp", tq)], w=[("ss", ssc + hb)])
                TTo("dve", lamb[:, 264 + ssc + 2:264 + ssc + 3], lamb[:, 264 + ssc:264 + ssc + 1], lamb[:, 264 + ssc + 1:264 + ssc + 2],
                    ALU.add, r=[("ss", ssc), ("ss", ssc + 1)], w=[("ss", ssc + 2)])
                ACT(lamb[:, 264 + ssc + 3:264 + ssc + 4], lamb[:, 264 + ssc + 2:264 + ssc + 3], AF.Sqrt, r=[("ss", ssc + 2), "cst"],
                    w=[("ss", ssc + 3)], scale=1.0 / D, bias=epsc)
                RECIP(lamb[:, 264 + ssc + 3:264 + ssc + 4], lamb[:, 264 + ssc + 3:264 + ssc + 4], w=[("ss", ssc + 3)])
                oi = tt % 2
                for hb, bk in enumerate((b0, b1)):
                    STT("dve", ost[oi][:, hb * 512:(hb + 1) * 512], ps[bk][:, :], lamb[:, 264 + ssc + 3:264 + ssc + 4],
                        gfin[:, hb * 512:(hb + 1) * 512], ALU.mult, ALU.mult, r=[("ss", ssc + 3), "gfin"],
                        w=[psk(bk), ("ost", oi)])
                DMA("sp", out[s0:s0 + 128, :], ost[oi], gout[oi], r=[("ost", oi)], w=[("out", tt)])
            P.op("sp", lambda e: e.nop(), r=[("out", tt) for tt in range(16)])

        if do0:
            layer_A()
            P.barrier()
            ffn(0)
            P.barrier()
        if mode == "L0":
            DMA("sp", hmid[:, :], hT[:, :], gmisc, r=ALLHT, w=["hmid"])
            P.op("sp", lambda e: e.nop(), r=["hmid"])
        if do1:
            layer_B()
            P.barrier()
            ffn(1)
            P.barrier()
            final()
        P.mark("end")
        if PRINT_MARKS:
            print("MARKS", P.marks)
        P.finalize(nc, st)
        P.run_block(nc)
    return nc


def _rot_tables():
    pos = np.concatenate([np.arange(NMETA, NMETA + SEQ), np.arange(NMETA)]).astype(np.float32)
    f32 = np.float32

    def inv(theta, half):
        return (f32(1.0) / np.power(f32(theta), np.arange(half, dtype=np.float32) / f32(half))).astype(np.float32)

    tabA = np.zeros((2, 128, T), np.float32); tabA[0] = 1.0
    ia = inv(500000.0, 8)
    for r in range(128):
        d = r % 64
        if d < 16:
            ang = (pos * ia[d % 8]).astype(np.float32)
            tabA[0, r] = np.cos(ang)
            tabA[1, r] = -np.sin(ang) if d < 8 else np.sin(ang)
    tabB = np.zeros((2, 128, T), np.float32); tabB[0] = 1.0
    ib = inv(10000.0, 16)
    for r in range(64, 96):
        j = r - 64
        ang = (pos * ib[j % 16]).astype(np.float32)
        tabB[0, r] = np.cos(ang)
        tabB[1, r] = -np.sin(ang) if j < 16 else np.sin(ang)
    return tabA, tabB


def _prep(inputs):
    f = lambda a: np.ascontiguousarray(np.asarray(a, dtype=np.float32))
    I = {k: f(v) for k, v in inputs.items()}
    tabA, tabB = _rot_tables()
    w_qkv = I["a_w_qkv"][0]
    idx = np.arange(2048)
    d = idx % 64
    idx_sw = np.where(d < 8, idx + 8, np.where(d < 16, idx - 8, idx))
    w_qk_sw = np.ascontiguousarray(w_qkv[:, idx_sw])
    w_uq = I["b_w_uq"][0]
    idx = np.arange(1536)
    j = idx % 96
    idx_sw = np.where((j >= 64) & (j < 80), idx + 16, np.where(j >= 80, idx - 16, idx))
    w_uq_sw = np.ascontiguousarray(w_uq[:, idx_sw])
    w_dkv = I["kv_w_dkv"]
    w_dkv_r = np.zeros((D, 96), np.float32); w_dkv_r[:, 64:96] = w_dkv[:, 256:288]
    w_dkv_rs = np.zeros((D, 96), np.float32); w_dkv_rs[:, 64:96] = w_dkv[:, 256 + (np.arange(32) + 16) % 32]
    vecs1 = np.concatenate([
        I["a_attn_norm"][0].reshape(8, 128), I["ffn_norm"][0].reshape(8, 128), I["kv_norm"].reshape(8, 128),
        I["b_attn_norm"][0].reshape(8, 128), I["ffn_norm"][1].reshape(8, 128), I["kv_a_norm"].reshape(2, 128),
        I["b_q_a_norm"][0].reshape(3, 128), I["a_sub_norm"][0].reshape(1, 128),
        I["ffn_conv_b"][0].reshape(22, 128), I["ffn_conv_b"][1].reshape(22, 128)], axis=0)
    vecs2 = I["ffn_conv_w"][0].reshape(66, 128)
    vecs3 = I["ffn_conv_w"][1].reshape(66, 128)
    lam4 = np.stack([I["a_lambda_q1"][0], I["a_lambda_k1"][0], I["a_lambda_q2"][0], I["a_lambda_k2"][0]], axis=0)
    shared0 = dict(meta=I["meta_tokens"], w_qkv=w_qkv, w_qk_sw=w_qk_sw, a_w_o=I["a_w_o"][0], lam4=f(lam4), tabA=tabA)
    shared1 = dict(w_dkv=w_dkv, w_dkv_r=w_dkv_r, w_dkv_rs=w_dkv_rs, w_ukv=I["kv_w_ukv"], w_dq=I["b_w_dq"][0],
                   w_uq=w_uq, w_uq_sw=w_uq_sw, b_w_o=I["b_w_o"][0], tabB=tabB, final_norm=I["final_norm"])
    sharedf = dict(w_in=I["ffn_w_in"], w_out=I["ffn_w_out"], vecs1=f(vecs1), vecs2=f(vecs2), vecs3=f(vecs3))
    return I, shared0, shared1, sharedf


FUSED = True
_NC_CACHE = {}


def _get_nc(mode):
    if mode not in _NC_CACHE:
        _NC_CACHE[mode] = build_nc(mode)
    return _NC_CACHE[mode]


def kernel(**inputs):
    I, s0, s1, sf = _prep(inputs)
    B = I["x"].shape[0]
    cores = list(range(B))
    if FUSED:
        nc = _get_nc("ALL")
        maps = [dict(x=I["x"][b], **s0, **s1, **sf) for b in range(B)]
        res = run_bass_kernel_spmd(nc, maps, core_ids=cores)
        return np.stack([np.asarray(r["out"], dtype=np.float32) for r in res.results], axis=0)
    nc0 = _get_nc("L0")
    maps = [dict(x=I["x"][b], **s0, **sf) for b in range(B)]
    res0 = run_bass_kernel_spmd(nc0, maps, core_ids=cores)
    nc1 = _get_nc("L1")
    maps = [dict(hmid=np.asarray(res0.results[b]["hmid"], dtype=np.float32), **s1, **sf) for b in range(B)]
    res1 = run_bass_kernel_spmd(nc1, maps, core_ids=cores)
    return np.stack([np.asarray(r["out"], dtype=np.float32) for r in res1.results], axis=0)
```

```python
import contextlib
import numpy as np
import concourse.bass as bass
import concourse.mybir as mybir
from concourse.bass_utils import run_bass_kernel_spmd

F32 = mybir.dt.float32
BF16 = mybir.dt.bfloat16
AF = mybir.ActivationFunctionType
ALU = mybir.AluOpType
AX = mybir.AxisListType

ENGS = ["pe", "act", "dve", "pool", "sp"]
SEM_CH = 4000


class DmaGroup:
    def __init__(self, gid):
        self.gid = gid
        self.count = 0
        self.sem = None


class Prog:
    def __init__(self):
        self.ops = {e: [] for e in ENGS}
        self.last_w = {}
        self.readers = {}
        self.waited = {c: {e: -1 for e in ENGS} for c in ENGS}
        self.waited_dma = {c: {} for c in ENGS}
        self.groups = []
        self.pending = {e: [] for e in ENGS}
        self.marks = []

    def dma_group(self):
        g = DmaGroup(len(self.groups))
        self.groups.append(g)
        return g

    def op(self, eng, fn, r=(), w=(), dma=None):
        idx = len(self.ops[eng])
        o = dict(eng=eng, fn=fn, waits=[], idx=idx, sig=False, dma=None)
        deps = []
        for k in r:
            t = self.last_w.get(k)
            if t is not None:
                deps.append(t)
        for k in w:
            t = self.last_w.get(k)
            if t is not None:
                deps.append(t)
            deps.extend(self.readers.get(k, ()))
        deps.extend(self.pending[eng])
        self.pending[eng] = []
        best_e, best_d = {}, {}
        for t in deps:
            if t[0] == "e":
                if t[2] > best_e.get(t[1], -1):
                    best_e[t[1]] = t[2]
            else:
                if t[2] > best_d.get(t[1].gid, (None, 0))[1]:
                    best_d[t[1].gid] = (t[1], t[2])
        for e, i in best_e.items():
            if e == eng:
                if eng in ("pe", "sp"):
                    continue
            if i <= self.waited[eng][e]:
                continue
            self.waited[eng][e] = i
            self.ops[e][i]["sig"] = True
            o["waits"].append(("e", e, i))
        for gid, (g, v) in best_d.items():
            if v <= self.waited_dma[eng].get(gid, 0):
                continue
            self.waited_dma[eng][gid] = v
            o["waits"].append(("d", g, v))
        if dma is not None:
            dma.count += 16
            o["dma"] = dma
            tok = ("d", dma, dma.count)
        else:
            tok = ("e", eng, idx)
        for k in w:
            self.last_w[k] = tok
            self.readers[k] = []
        for k in r:
            if k in w:
                continue
            self.readers.setdefault(k, []).append(tok)
        self.ops[eng].append(o)
        return tok

    def mark(self, name):
        self.marks.append(name)
        self.op("pe", "MARK")

    def barrier(self):
        toks = []
        for e in ENGS:
            if self.ops[e]:
                for o in reversed(self.ops[e]):
                    if o["dma"] is None and o["fn"] != "MARK":
                        toks.append(("e", e, o["idx"]))
                        break
        for g in self.groups:
            if g.count:
                toks.append(("d", g, g.count))
        for e in ENGS:
            self.pending[e] = list(toks)

    def finalize(self, nc, stack):
        self.nsig = {}
        for e in ENGS:
            c = 0
            for o in self.ops[e]:
                if o["sig"]:
                    c += 1
                    o["sigval"] = c
            self.nsig[e] = c
        self.esems = {}
        for e in ENGS:
            n = (self.nsig[e] + SEM_CH - 1) // SEM_CH
            self.esems[e] = [stack.enter_context(nc.semaphore(f"s_{e}{i}")) for i in range(n)]
        for g in self.groups:
            if g.count:
                g.sem = stack.enter_context(nc.semaphore(f"d_{g.gid}"))
        self.mark_sem = stack.enter_context(nc.semaphore("phase_mark")) if self.marks else None

    def emit_engine(self, ename, eng):
        for o in self.ops[ename]:
            for t in o["waits"]:
                if t[0] == "e":
                    _, e, i = t
                    sv = self.ops[e][i]["sigval"]
                    sem = self.esems[e][(sv - 1) // SEM_CH]
                    val = (sv - 1) % SEM_CH + 1
                    eng.wait_ge(sem, val)
                else:
                    _, g, v = t
                    eng.wait_ge(g.sem, v)
            if o["fn"] == "MARK":
                eng.nop().then_inc(self.mark_sem, 1)
                continue
            ins = o["fn"](eng)
            if o["dma"] is not None:
                ins.then_inc(o["dma"].sem, 16)
            elif o["sig"]:
                sv = o["sigval"]
                ins.then_inc(self.esems[ename][(sv - 1) // SEM_CH], 1)

    def run_block(self, nc):
        with nc.Block() as block:
            @block.tensor
            def _(e):
                self.emit_engine("pe", e)

            @block.scalar
            def _(e):
                self.emit_engine("act", e)

            @block.vector
            def _(e):
                self.emit_engine("dve", e)

            @block.gpsimd
            def _(e):
                self.emit_engine("pool", e)

            @block.sync
            def _(e):
                self.emit_engine("sp", e)


D = 1024
SEQ = 2048
NMETA = 16
T = SEQ + NMETA
EPS = 1e-5
GR = [(0, 512), (512, 512), (1024, 512), (1536, 512), (2048, 16)]
TT = [(i * 128, 128) for i in range(16)] + [(2048, 16)]
DFF = 2816
FBLOCKS = [list(range(0, 8)), list(range(8, 15)), list(range(15, 22))]
LAM_INIT0 = 0.8 - 0.6 * 1.0
T6 = [0, 1, 2, 3, 4, 5]
T4 = [0, 1, 2, 3]
PRINT_MARKS = False


def build_nc(mode):
    nc = bass.Bass("TRN2", target_bir_lowering=False)
    dt_in = lambda n, s: nc.dram_tensor(n, s, F32, kind="ExternalInput").ap()
    dt_out = lambda n, s: nc.dram_tensor(n, s, F32, kind="ExternalOutput").ap()
    do0 = mode in ("ALL", "L0")
    do1 = mode in ("ALL", "L1")
    if do0:
        x = dt_in("x", [SEQ, D]); meta = dt_in("meta", [NMETA, D])
        w_qkv = dt_in("w_qkv", [D, 3072]); w_qk_sw = dt_in("w_qk_sw", [D, 2048])
        a_w_o = dt_in("a_w_o", [D, D]); lam4 = dt_in("lam4", [4, 64]); tabA = dt_in("tabA", [2, 128, T])
    if do1:
        w_dkv = dt_in("w_dkv", [D, 288]); w_dkv_r = dt_in("w_dkv_r", [D, 96]); w_dkv_rs = dt_in("w_dkv_rs", [D, 96])
        w_ukv = dt_in("w_ukv", [256, 2048]); w_dq = dt_in("w_dq", [D, 384])
        w_uq = dt_in("w_uq", [384, 1536]); w_uq_sw = dt_in("w_uq_sw", [384, 1536])
        b_w_o = dt_in("b_w_o", [D, D]); tabB = dt_in("tabB", [2, 128, T]); fin = dt_in("final_norm", [D])
    w_in = dt_in("w_in", [2, D, 2 * DFF]); w_out = dt_in("w_out", [2, DFF, D])
    vecs1 = dt_in("vecs1", [90, 128]); vecs2 = dt_in("vecs2", [66, 128]); vecs3 = dt_in("vecs3", [66, 128])
    if mode == "L0":
        hmid = dt_out("hmid", [128, 8 * T])
    if mode == "L1":
        hmid = dt_in("hmid", [128, 8 * T])
    if do1:
        out = dt_out("out", [SEQ, D])

    P = Prog()
    with contextlib.ExitStack() as st:
        sb = lambda n, s, d: st.enter_context(nc.sbuf_tensor(n, s, d))
        hT = sb("hT", [128, 8 * T], F32)
        xn = sb("xn", [128, 8 * T], BF16)
        arF = sb("arF", [128, 2 * (T + 2)], F32)
        arB = sb("arB", [128, 16736], BF16)
        tmp = [sb(f"tmp{i}", [128, 512], F32) for i in range(6)]
        wp = [sb(f"wp{i}", [128, 1024], BF16) for i in range(8)]
        ckvn = sb("ckvn", [128, 2 * T], BF16)
        pt = [sb(f"pt{i}", [128, 512], BF16) for i in range(4)]
        sq = [sb(f"sq{i}", [128, 512], BF16) for i in range(4)]
        rs = [sb(f"rs{i}", [128, 512], F32) for i in range(2)]
        sqe = [sb(f"sqe{i}", [128, 512], BF16) for i in range(2)]
        ocm = [sb(f"ocm{i}", [128, 512], F32) for i in range(2)]
        gfin = sb("gfin", [128, 1024], F32)
        ident = sb("ident", [128, 128], F32)
        ones = sb("ones", [128, 128], BF16)
        VT1 = sb("VT1", [128, 90], F32)
        VT2 = sb("VT2", [128, 66], F32)
        VT3 = sb("VT3", [128, 66], F32)
        cst = sb("cst", [128, 8], F32)
        lamb = sb("lamb", [128, 272], F32)
        ps = [st.enter_context(nc.psum_tensor(f"ps{i}", [128, 512], F32)) for i in range(8)]

        hT3 = hT[:, :].rearrange("p (k t) -> p k t", k=8)
        xn3 = xn[:, :].rearrange("p (k t) -> p k t", k=8)
        ckvn3 = ckvn[:, :].rearrange("p (k t) -> p k t", k=2)
        TC = arF[:, 0:T]
        TS_ = arF[:, T + 2:2 * T + 2]
        AR = [arF[:, 0:T + 2], arF[:, T + 2:2 * T + 4]]
        QB = [arB[:, 0:T], arB[:, T:2 * T]]
        KB = [arB[:, 2 * T:3 * T], arB[:, 3 * T:4 * T]]
        VB = [arB[:, 4 * T + i * 2176:4 * T + (i + 1) * 2176].rearrange("p (t c) -> p t c", c=128) for i in range(2)]
        oTb = [arB[:, 4 * T + 4352 + i * T:4 * T + 4352 + (i + 1) * T] for i in range(2)]
        gated3 = arB[:, 0:8 * T].rearrange("p (j t) -> p j t", j=8)
        epsc = cst[:, 0:1]

        def psk(b):
            return ("ps", b)

        def MM(o, lhsT, rhs, start, stop, r=(), w=()):
            P.op("pe", lambda e: e.matmul(o, lhsT, rhs, start=start, stop=stop), r=r, w=w)

        def ACT(o, i, func, r=(), w=(), scale=1.0, bias=None):
            if bias is None:
                P.op("act", lambda e: e.activation(out=o, in_=i, func=func, scale=scale), r=r, w=w)
            else:
                P.op("act", lambda e: e.activation(out=o, in_=i, func=func, scale=scale, bias=bias), r=r, w=w)

        def TTo(eng, o, a, b, op, r=(), w=()):
            P.op(eng, lambda e: e.tensor_tensor(out=o, in0=a, in1=b, op=op), r=r, w=w)

        def STT(eng, o, a, s, b, op0, op1, r=(), w=()):
            P.op(eng, lambda e: e.scalar_tensor_tensor(out=o, in0=a, scalar=s, in1=b, op0=op0, op1=op1), r=r, w=w)

        def TS(eng, o, a, s1, s2, op0, op1, r=(), w=()):
            P.op(eng, lambda e: e.tensor_scalar(out=o, in0=a, scalar1=s1, scalar2=s2, op0=op0, op1=op1), r=r, w=w)

        def RECIP(o, i, r=(), w=()):
            nf = o.shape[-1]
            if nf < 64:
                P.op("dve", lambda e: e.reciprocal(out=o, in_=i), r=r, w=w)
                return
            ACT(o, i, AF.Ln, r=r, w=w)
            ACT(o, o, AF.Exp, w=[k for k in w if k[0] != "ps"], scale=-1.0)

        def RSTD(o, i, scale, r=(), w=()):
            ACT(o, i, AF.Ln, r=list(r) + ["cst"], w=w, scale=scale, bias=epsc)
            ACT(o, o, AF.Exp, w=[k for k in w if k[0] != "ps"], scale=-0.5)

        def COPY(eng, o, i, r=(), w=()):
            if eng == "act":
                P.op("act", lambda e: e.activation(out=o, in_=i, func=AF.Copy), r=r, w=w)
            else:
                P.op(eng, lambda e: e.tensor_copy(out=o, in_=i), r=r, w=w)

        def MEMSET(eng, o, val, r=(), w=()):
            P.op(eng, lambda e: e.memset(o, val), r=r, w=w)

        def DMA(q, o, i, grp, r=(), w=()):
            P.op(q, lambda e: e.dma_start(out=o, in_=i), r=r, w=w, dma=grp)

        wgrp = [P.dma_group() for _ in range(8)]
        wctr = [0]

        def LOADW(view_fn, src):
            i = wctr[0] % 8
            wctr[0] += 1
            v = view_fn(wp[i])
            DMA("pool", v, src, wgrp[i], w=[("w", i)])
            return v, ("w", i)

        def kview(kt, m):
            return lambda buf: buf[:, 0:kt * m].rearrange("p (k m) -> p k m", k=kt)

        def ksrc(ap2d):
            return ap2d.rearrange("(k p) m -> p k m", p=128)

        rot = {}

        def nxt(name, lst):
            i = rot.get(name, 0)
            rot[name] = i + 1
            return lst[i % len(lst)]

        gmisc = P.dma_group()
        gtab = P.dma_group()
        gx = [P.dma_group() for _ in range(4)]
        gout = [P.dma_group() for _ in range(4)]

        MEMSET("pool", ident[:, :], 0.0, w=["ident"])
        P.op("pool", lambda e: e.affine_select(out=ident[:, :], in_=ident[:, :], pattern=[[-1, 128]],
                                                compare_op=ALU.not_equal, fill=1.0, base=0, channel_multiplier=1),
             r=["ident"], w=["ident"])
        MEMSET("pool", ones[:, :], 1.0, w=["ones"])
        MEMSET("pool", cst[:, 0:1], EPS, w=["cst"])
        MEMSET("pool", cst[0:64, 1:2], 0.0, w=["cst"])
        MEMSET("pool", cst[64:128, 1:2], -30000.0, w=["cst"])
        vsrc = ((vecs1, 90, VT1, "VT1"), (vecs2, 66, VT2, "VT2"), (vecs3, 66, VT3, "VT3"))
        gv = [P.dma_group() for _ in range(3)]
        for vi, (src, R, dst, nm) in enumerate(vsrc):
            DMA("sp", tmp[vi][0:R, 0:128], src[:, :], gv[vi], w=[("tmp", vi)])
        for vi, (src, R, dst, nm) in enumerate(vsrc):
            stg = tmp[vi][0:R, 0:128]
            P.op("pe", lambda e, stg=stg, R=R, vi=vi: e.transpose(ps[5 + vi][:, 0:R], stg, ident[0:R, 0:R]),
                 r=[("tmp", vi), "ident"], w=[psk(5 + vi)])
            COPY("dve", dst[:, :], ps[5 + vi][:, 0:R], w=[psk(5 + vi), nm])
        VG = {"a_attn": 0, "ffn0": 8, "kv": 16, "b_attn": 24, "ffn1": 32, "kv_a": 40, "q_a": 42, "sub": 45,
              "cb0": 46, "cb1": 68}
        ALLHT = [("hT", k, g) for k in range(8) for g in range(5)]

        if do0:
            xs = [arF[:, i * 1024:(i + 1) * 1024] for i in range(4)]
            for g in range(4):
                for t4 in range(4):
                    tt = g * 4 + t4
                    DMA("sp", xs[t4], x[tt * 128:(tt + 1) * 128, :], gx[t4], w=[("xs", t4)])
                    for k in range(8):
                        P.op("pe", lambda e, k=k, t4=t4: e.transpose(ps[k][:, t4 * 128:(t4 + 1) * 128],
                                                                     xs[t4][:, k * 128:(k + 1) * 128], ident[:, :]),
                             r=[("xs", t4), "ident"], w=[psk(k)])
                for k in range(8):
                    COPY("act" if k % 2 else "dve", hT3[:, k, g * 512:(g + 1) * 512], ps[k][:, :],
                         w=[psk(k), ("hT", k, g)])
            DMA("sp", xs[0][0:16, :], meta[:, :], gx[0], w=[("xs", 0)])
            for k in range(8):
                P.op("pe", lambda e, k=k: e.transpose(ps[k][:, 0:16], xs[0][0:16, k * 128:(k + 1) * 128], ident[0:16, 0:16]),
                     r=[("xs", 0), "ident"], w=[psk(k)])
                COPY("act" if k % 2 else "dve", hT3[:, k, 2048:2064], ps[k][:, 0:16], w=[psk(k), ("hT", k, 4)])
        else:
            DMA("sp", hT[:, :], hmid[:, :], gmisc, w=ALLHT)
        P.barrier()

        def norm_phase(gcol, order=(0, 1, 2, 3, 4)):
            P.mark(f"norm{gcol}")
            for g in order:
                s0, n = GR[g]
                nb = nxt("pN", [6, 7])
                for k in range(8):
                    sqi = nxt("sq", [0, 1, 2, 3])
                    ACT(sq[sqi][:, :n], hT3[:, k, s0:s0 + n], AF.Square, r=[("hT", k, g)], w=[("sq", sqi)])
                    MM(ps[nb][:, :n], ones[:, :], sq[sqi][:, :n], k == 0, k == 7, r=["ones", ("sq", sqi)], w=[psk(nb)])
                ri = nxt("rs", [0, 1])
                RSTD(rs[ri][:, :n], ps[nb][:, :n], 1.0 / D, w=[psk(nb), ("rs", ri)])
                for k in range(8):
                    STT("dve", xn3[:, k, s0:s0 + n], hT3[:, k, s0:s0 + n], VT1[:, gcol + k:gcol + k + 1], rs[ri][:, :n],
                        ALU.mult, ALU.mult, r=[("hT", k, g), ("rs", ri), "VT1"], w=[("xn", k, g)])

        def _pv(pd, ob, sbk, n, nk, Vv, vkey, extra_r=()):
            pti, kt, kn, qlo, idx = pd
            MM(ps[ob][:, qlo:n], Vv[:kn, kt, :], ptl[pti][:kn, qlo:n], idx == 0, idx == nk - 1,
               r=[("pt", pti), vkey] + list(extra_r), w=[psk(ob)])
            if sbk is not None:
                MM(ps[sbk][:, qlo:n], ones[:kn, :], ptl[pti][:kn, qlo:n], idx == 0, idx == nk - 1,
                   r=[("pt", pti), "ones"], w=[psk(sbk)])

        CHUNK = 2
        chunk_cfg = [2]
        ptl = [pt[0], pt[1], pt[2], pt[3]]
        SBANKS = [0, 1, 6, 7]
        obanks = [[2, 3, 4, 5]]

        def attention_unit(maps, scale, evac_map, combine, sep_sums, fillers=(), carry=None, last=True):
            fillers = list(fillers)
            deferred = list(carry) if carry else []
            chain = [0]

            def run_deferred(item):
                d = item[1]()
                if d is not None:
                    deferred.insert(0, [item[0], d])

            def pop_filler():
                f = fillers.pop(0)
                if getattr(f, "needs_flush", False):
                    while deferred and deferred[0][0] == 0:
                        run_deferred(deferred.pop(0))
                f()

            CH = chunk_cfg[0]
            steps_left = [len(maps) * sum(-(-(5 + 4 * g) // CH) - 1 for g in range(4))]
            for g in [4, 0, 1, 2, 3]:
                s0, n = GR[g]
                kts = [16] if g == 4 else [16] + list(range(0, 4 * g + 4))
                nk = len(kts)
                chunks = [list(range(i, min(i + CH, nk))) for i in range(0, nk, CH)]
                for mi, (Kb, Qb, r0, nr, kkeys, qkey, Vv, vkey) in enumerate(maps):
                    chain[0] += 1
                    while deferred and deferred[0][0] < chain[0] - 1:
                        run_deferred(deferred.pop(0))
                    ob = nxt("pO", obanks[0])
                    sbk = nxt("pO", obanks[0]) if sep_sums else None

                    def emit_pv(chunk, ob=ob, sbk=sbk, n=n, nk=nk, Vv=Vv, vkey=vkey):
                        keys = [("pt", c[0]) for c in chunk] + [("ptm", c[0]) for c in chunk]
                        for ci, c in enumerate(chunk):
                            _pv(c, ob, sbk, n, nk, Vv, vkey, extra_r=keys if ci == 0 else ())

                    prev = None
                    for ch in chunks:
                        cur = []
                        for idx in ch:
                            kt = kts[idx]
                            ks0, kn = TT[kt]
                            qlo = 0
                            diag = g < 4 and kt != 16 and kt >= 4 * g
                            if diag:
                                qlo = 128 * (kt - 4 * g)
                            sbank = nxt("pS", SBANKS)
                            pti = nxt("pt", list(range(len(ptl))))
                            MM(ps[sbank][:kn, qlo:n], Kb[r0:r0 + nr, ks0:ks0 + kn], Qb[r0:r0 + nr, s0 + qlo:s0 + n], True, True,
                               r=list(kkeys) + [qkey(g)], w=[psk(sbank)])
                            if diag:
                                ACT(ptl[pti][:kn, qlo + 64:n], ps[sbank][:kn, qlo + 64:n], AF.Exp, r=[psk(sbank)],
                                    w=[("pt", pti)], scale=scale)
                                ACT(ptl[pti][:kn, qlo:qlo + 64], ps[sbank][:kn, qlo:qlo + 64], AF.Exp, r=[psk(sbank), "cst"],
                                    w=[("ptm", pti)], scale=scale, bias=cst[:kn, 1:2])
                            else:
                                ACT(ptl[pti][:kn, qlo:n], ps[sbank][:kn, qlo:n], AF.Exp, r=[psk(sbank)],
                                    w=[("pt", pti), ("ptm", pti)], scale=scale)
                            cur.append((pti, kt, kn, qlo, idx))
                        if prev is None:
                            for _ in range(min(2, len(fillers))):
                                pop_filler()
                        if prev is not None:
                            emit_pv(prev)
                            steps_left[0] -= 1
                            if deferred:
                                run_deferred(deferred.pop(0))
                            npop = -(-len(fillers) // max(steps_left[0] + 1, 1))
                            for _ in range(min(npop, len(fillers))):
                                pop_filler()
                        prev = cur
                    emit_pv(prev)
                    deferred.append([chain[0], (lambda mi=mi, g=g, s0=s0, n=n, ob=ob, sbk=sbk: evac_map(mi, g, s0, n, ob, sbk))])
                if combine is not None:
                    deferred.append([chain[0], (lambda g=g, s0=s0, n=n: combine(g, s0, n))])
            while fillers:
                pop_filler()
            if last:
                while deferred:
                    run_deferred(deferred.pop(0))
                return []
            return [[0, d[1]] for d in deferred]

        def wo_tiles(wo, wk, oT_slot, okeys):
            tiles = []
            for g in [4, 0, 1, 2, 3]:
                s0, n = GR[g]
                for m in range(8):
                    def f(m=m, g=g, s0=s0, n=n):
                        b = nxt("pS", SBANKS)
                        MM(ps[b][:, :n], wo[:, m * 128:(m + 1) * 128], oTb[oT_slot][:, s0:s0 + n], True, True,
                           r=[wk] + okeys(g), w=[psk(b)])
                        TTo("dve", hT3[:, m, s0:s0 + n], ps[b][:, :n], hT3[:, m, s0:s0 + n], ALU.add,
                            w=[psk(b), ("hT", m, g)])
                    f.needs_flush = True
                    tiles.append(f)
            return tiles

        def proj_rot(dst, wfn, wk, wsfn, wsk, ktiles, src3, srckey, rows, dkey, lazy=False):
            out = []
            for g, (s0, n) in enumerate(GR):
                def f(g=g, s0=s0, n=n):
                    if lazy:
                        b1 = nxt("pS", SBANKS)
                        b2 = nxt("pS", SBANKS)
                    else:
                        b1, b2 = nxt("pP", [(0, 1), (6, 7)])
                    for k in range(ktiles):
                        MM(ps[b1][0:rows, :n], wfn(k), src3[:, k, s0:s0 + n], k == 0, k == ktiles - 1,
                           r=[wk, (srckey, k, g)], w=[psk(b1)])
                    for k in range(ktiles):
                        MM(ps[b2][0:rows, :n], wsfn(k), src3[:, k, s0:s0 + n], k == 0, k == ktiles - 1,
                           r=[wsk, (srckey, k, g)], w=[psk(b2)])
                    t1 = nxt("tmp", T4)
                    t2 = nxt("tmp", T4)
                    TTo("dve", tmp[t1][0:rows, :n], ps[b1][0:rows, :n], TC[0:rows, s0:s0 + n], ALU.mult,
                        r=["tab"], w=[psk(b1), ("tmp", t1)])
                    TTo("dve", tmp[t2][0:rows, :n], ps[b2][0:rows, :n], TS_[0:rows, s0:s0 + n], ALU.mult,
                        r=["tab"], w=[psk(b2), ("tmp", t2)])
                    TTo("pool" if lazy else "dve", dst[0:rows, s0:s0 + n], tmp[t1][0:rows, :n], tmp[t2][0:rows, :n], ALU.add,
                        r=[("tmp", t1), ("tmp", t2)], w=[dkey(g)])
                if lazy:
                    out.append(f)
                else:
                    f()
            return out

        def interleave(a, b):
            out = []
            na, nb_ = len(a), len(b)
            ia = ib = 0
            while ia < na or ib < nb_:
                if ib >= nb_ or (ia < na and ia * nb_ <= ib * na):
                    out.append(a[ia]); ia += 1
                else:
                    out.append(b[ib]); ib += 1
            return out


        def layer_A():
            DMA("sp", TC, tabA[0], gtab, w=["tab"])
            DMA("sp", TS_, tabA[1], gtab, w=["tab"])
            for i in range(4):
                DMA("sp", lamb[:, i * 64:(i + 1) * 64], lam4[i].partition_broadcast(128), gmisc, w=["lamb"])
            def load_proj(h):
                c = h * 128
                W = {}
                for nm, c0 in (("Q", c), ("K", 1024 + c)):
                    W[nm] = LOADW(kview(8, 128), ksrc(w_qkv[:, c0:c0 + 128]))
                    W[nm + "s"] = LOADW(kview(8, 128), ksrc(w_qk_sw[:, c0:c0 + 128]))
                W["V"] = LOADW(kview(8, 128), ksrc(w_qkv[:, 2048 + c:2048 + c + 128]))
                return W

            W0 = load_proj(0)
            norm_phase(VG["a_attn"])
            TTo("dve", lamb[:, 0:64], lamb[:, 0:64], lamb[:, 64:128], ALU.mult, w=["lamb"])
            TTo("dve", lamb[:, 128:192], lamb[:, 128:192], lamb[:, 192:256], ALU.mult, w=["lamb"])
            P.op("dve", lambda e: e.reduce_sum(out=lamb[:, 256:257], in_=lamb[:, 0:64], axis=AX.X), w=["lamb"])
            P.op("dve", lambda e: e.reduce_sum(out=lamb[:, 257:258], in_=lamb[:, 128:192], axis=AX.X), w=["lamb"])
            ACT(lamb[:, 258:260], lamb[:, 256:258], AF.Exp, w=["lamb"])
            TTo("dve", lamb[:, 260:261], lamb[:, 258:259], lamb[:, 259:260], ALU.subtract, w=["lamb"])
            TS("dve", lamb[:, 261:262], lamb[:, 260:261], LAM_INIT0, -1.0, ALU.add, ALU.mult, w=["lamb"])
            TS("dve", lamb[:, 262:263], VT1[:, VG["sub"]:VG["sub"] + 1], 1.0 - LAM_INIT0, 0.0, ALU.mult, ALU.add,
               r=["VT1"], w=["lamb"])
            neglam = lamb[:, 261:262]
            gsub = lamb[:, 262:263]


            def proj_unit(h, W, lazy):
                slot = h % 2
                Qb, Kb, Vv = QB[slot], KB[slot], VB[slot]
                out = []
                for (dst, nm) in ((Qb, "Q"), (Kb, "K")):
                    (wv, wk), (wsv, wsk) = W[nm], W[nm + "s"]
                    out += proj_rot(dst, (lambda k, wv=wv: wv[:, k, :]), wk, (lambda k, wsv=wsv: wsv[:, k, :]), wsk,
                                    8, xn3, "xn", 128, (lambda g, nm=nm, slot=slot: (nm + "f", slot, g)), lazy=lazy)
                wvv, wvk = W["V"]
                vkey = ("Vf", slot)
                for t0 in range(0, 17, 4):
                    def fv(t0=t0):
                        vb = nxt("pS", SBANKS) if lazy else nxt("pO", [2, 3, 4, 5])
                        for tt in range(t0, min(t0 + 4, 17)):
                            s0, n = TT[tt]
                            c0 = (tt % 4) * 128
                            g = min(tt // 4, 4)
                            for k in range(8):
                                MM(ps[vb][:n, c0:c0 + 128], xn3[:, k, s0:s0 + n], wvv[:, k, :], k == 0, k == 7,
                                   r=[wvk, ("xn", k, g)], w=[psk(vb)])
                        if t0 < 16:
                            COPY("act", Vv[:, t0:t0 + 4, :], ps[vb][:, :].rearrange("p (t c) -> p t c", c=128),
                                 w=[psk(vb), vkey])
                        else:
                            COPY("act", Vv[0:16, 16, :], ps[vb][0:16, 0:128], w=[psk(vb), vkey])
                    if lazy:
                        out.append(fv)
                    else:
                        fv()
                return out

            P.mark("A0.proj")
            proj_unit(0, W0, False)
            prev_fill = []
            carryA = []
            for h in range(8):
                slot = h % 2
                Qb, Kb, Vv = QB[slot], KB[slot], VB[slot]
                vkey = ("Vf", slot)
                next_proj = []
                if h + 1 < 8:
                    next_proj = proj_unit(h + 1, load_proj(h + 1), True)
                wo, wok = LOADW(lambda b: b[:, :], a_w_o[h * 128:(h + 1) * 128, :])
                this_fill = wo_tiles(wo, wok, slot, (lambda g, slot=slot: [("oT", slot, g)]))
                prev_fill = interleave(prev_fill, next_proj)

                o_t = {}

                def evac_map(mi, g, s0, n, ob, sbk, o_t=o_t):
                    to = 4 + mi
                    RECIP(tmp[to][:, :n], ps[sbk][:, :n], w=[psk(sbk), ("tmp", to)])
                    TTo("dve", tmp[to][:, :n], ps[ob][:, :n], tmp[to][:, :n], ALU.mult, w=[psk(ob), ("tmp", to)])
                    o_t[mi] = to

                def combine(g, s0, n, slot=slot, o_t=o_t):
                    t1, t2 = o_t[0], o_t[1]
                    STT("dve", tmp[t1][:, :n], tmp[t2][:, :n], neglam, tmp[t1][:, :n], ALU.mult, ALU.add,
                        r=[("tmp", t2), "lamb"], w=[("tmp", t1)])
                    sqi = nxt("sqe", [0, 1])
                    ACT(sqe[sqi][:, :n], tmp[t1][:, :n], AF.Square, r=[("tmp", t1)], w=[("sqe", sqi)])
                    oi = nxt("ocm", [0, 1])
                    COPY("pool", ocm[oi][:, :n], tmp[t1][:, :n], r=[("tmp", t1)], w=[("ocm", oi)])

                    def tail(g=g, s0=s0, n=n, sqi=sqi, oi=oi, slot=slot):
                        nb = nxt("pS", SBANKS)
                        MM(ps[nb][:, :n], ones[:, :], sqe[sqi][:, :n], True, True, r=["ones", ("sqe", sqi)], w=[psk(nb)])
                        ri = nxt("rs", [0, 1])
                        RSTD(rs[ri][:, :n], ps[nb][:, :n], 1.0 / 128, w=[psk(nb), ("rs", ri)])
                        STT("dve", oTb[slot][:, s0:s0 + n], ocm[oi][:, :n], gsub, rs[ri][:, :n], ALU.mult, ALU.mult,
                            r=[("ocm", oi), ("rs", ri), "lamb"], w=[("oT", slot, g)])
                    return tail

                P.mark(f"A{h}.attn")
                kk = [("Kf", slot, g) for g in range(5)]
                qk = lambda g, slot=slot: ("Qf", slot, g)
                carryA = attention_unit([(Kb, Qb, 0, 64, kk, qk, Vv, vkey), (Kb, Qb, 64, 64, kk, qk, Vv, vkey)], 64 ** -0.5,
                                        evac_map, combine, True, fillers=prev_fill, carry=carryA, last=(h == 7))
                prev_fill = this_fill
            P.mark("A.wo_last")
            for f in prev_fill:
                f()

        def layer_B():
            DMA("sp", TC, tabB[0], gtab, w=["tab"])
            DMA("sp", TS_, tabB[1], gtab, w=["tab"])
            wl0, wl0k = LOADW(kview(8, 128), ksrc(w_dkv[:, 0:128]))
            wl1, wl1k = LOADW(kview(8, 128), ksrc(w_dkv[:, 128:256]))
            wr, wrk = LOADW(kview(8, 96), ksrc(w_dkv_r[:, :]))
            wrs, wrsk = LOADW(kview(8, 96), ksrc(w_dkv_rs[:, :]))
            DMA("sp", gfin[:, :], fin.partition_broadcast(128), gmisc, w=["gfin"])
            norm_phase(VG["kv"])
            P.mark("B.kv")
            for g, (s0, n) in enumerate(GR):
                cb = [nxt("pO", [2, 3, 4, 5]) for _ in range(2)]
                for j, (wl, wlk) in enumerate(((wl0, wl0k), (wl1, wl1k))):
                    for k in range(8):
                        MM(ps[cb[j]][:, :n], wl[:, k, :], xn3[:, k, s0:s0 + n], k == 0, k == 7,
                           r=[wlk, ("xn", k, g)], w=[psk(cb[j])])
                nb = nxt("pN", [6, 7])
                for j in range(2):
                    sqi = nxt("sq", [0, 1, 2, 3])
                    ACT(sq[sqi][:, :n], ps[cb[j]][:, :n], AF.Square, w=[psk(cb[j]), ("sq", sqi)])
                    MM(ps[nb][:, :n], ones[:, :], sq[sqi][:, :n], j == 0, j == 1, r=["ones", ("sq", sqi)], w=[psk(nb)])
                ri = nxt("rs", [0, 1])
                RSTD(rs[ri][:, :n], ps[nb][:, :n], 1.0 / 256, w=[psk(nb), ("rs", ri)])
                for j in range(2):
                    STT("dve", ckvn3[:, j, s0:s0 + n], ps[cb[j]][:, :n], VT1[:, VG["kv_a"] + j:VG["kv_a"] + j + 1],
                        rs[ri][:, :n], ALU.mult, ALU.mult, r=[("rs", ri), "VT1"], w=[psk(cb[j]), ("ckvn", j, g)])
                b1, b2 = nxt("pP", [(0, 1)])
                for k in range(8):
                    MM(ps[b1][0:96, :n], wr[:, k, :], xn3[:, k, s0:s0 + n], k == 0, k == 7, r=[wrk, ("xn", k, g)], w=[psk(b1)])
                for k in range(8):
                    MM(ps[b2][0:96, :n], wrs[:, k, :], xn3[:, k, s0:s0 + n], k == 0, k == 7, r=[wrsk, ("xn", k, g)], w=[psk(b2)])
                t1 = nxt("tmp", T6)
                t2 = nxt("tmp", T6)
                TTo("dve", tmp[t1][64:96, :n], ps[b1][64:96, :n], TC[64:96, s0:s0 + n], ALU.mult,
                    r=["tab"], w=[psk(b1), ("tmp", t1)])
                TTo("dve", tmp[t2][64:96, :n], ps[b2][64:96, :n], TS_[64:96, s0:s0 + n], ALU.mult,
                    r=["tab"], w=[psk(b2), ("tmp", t2)])
                for hh in range(2):
                    TTo("pool", KB[hh][64:96, s0:s0 + n], tmp[t1][64:96, :n], tmp[t2][64:96, :n], ALU.add,
                        r=[("tmp", t1), ("tmp", t2)], w=[("krope", hh, g)])
            wq = [LOADW(kview(8, 128), ksrc(w_dq[:, j * 128:(j + 1) * 128])) for j in range(3)]
            norm_phase(VG["b_attn"])
            P.mark("B.dq")
            for g, (s0, n) in enumerate(GR):
                cb = [nxt("pO", [2, 3, 4, 5]) for _ in range(3)]
                for j in range(3):
                    for k in range(8):
                        MM(ps[cb[j]][:, :n], wq[j][0][:, k, :], xn3[:, k, s0:s0 + n], k == 0, k == 7,
                           r=[wq[j][1], ("xn", k, g)], w=[psk(cb[j])])
                nb = nxt("pN", [6, 7])
                for j in range(3):
                    sqi = nxt("sq", [0, 1, 2, 3])
                    ACT(sq[sqi][:, :n], ps[cb[j]][:, :n], AF.Square, w=[psk(cb[j]), ("sq", sqi)])
                    MM(ps[nb][:, :n], ones[:, :], sq[sqi][:, :n], j == 0, j == 2, r=["ones", ("sq", sqi)], w=[psk(nb)])
                ri = nxt("rs", [0, 1])
                RSTD(rs[ri][:, :n], ps[nb][:, :n], 1.0 / 384, w=[psk(nb), ("rs", ri)])
                for j in range(3):
                    STT("dve", xn3[:, j, s0:s0 + n], ps[cb[j]][:, :n], VT1[:, VG["q_a"] + j:VG["q_a"] + j + 1],
                        rs[ri][:, :n], ALU.mult, ALU.mult, r=[("rs", ri), "VT1"],
                        w=[psk(cb[j])] + [("xn", kk_, g) for kk_ in range(8)])
            MEMSET("pool", VB[0][:, :, 64:128], 1.0, w=[("Vf", 0)])
            MEMSET("pool", VB[1][:, :, 0:64], 1.0, w=[("Vf", 1)])

            def load_proj_B(u):
                W = {}
                W["q"] = LOADW(kview(3, 192), ksrc(w_uq[:, u * 192:(u + 1) * 192]))
                W["qs"] = LOADW(kview(3, 192), ksrc(w_uq_sw[:, u * 192:(u + 1) * 192]))
                W["kv"] = LOADW(kview(2, 256), ksrc(w_ukv[:, u * 256:(u + 1) * 256]))
                return W

            P.barrier()
            Qsets = [[QB[0], QB[1]], [xn3[:, 3, :], xn3[:, 4, :]]]
            Ksets = [[KB[0], KB[1]], [xn3[:, 5, :], xn3[:, 6, :]]]
            for hh in range(2):
                COPY("dve" if hh else "act", Ksets[1][hh][64:96, :], KB[hh][64:96, :], r=[("krope", hh, g) for g in range(5)],
                     w=[("krope2", hh)])

            def proj_B(u, W, lazy):
                st_ = u % 2
                (wq2, wq2k), (wq2s, wq2sk), (wkv2, wkv2k) = W["q"], W["qs"], W["kv"]
                out = []
                for hh in range(2):
                    out += proj_rot(Qsets[st_][hh], (lambda k, hh=hh, wq2=wq2: wq2[:, k, hh * 96:(hh + 1) * 96]), wq2k,
                                    (lambda k, hh=hh, wq2s=wq2s: wq2s[:, k, hh * 96:(hh + 1) * 96]), wq2sk,
                                    3, xn3, "xn", 96, (lambda g, hh=hh, st_=st_: ("Qf", st_, hh, g)), lazy=lazy)
                    for g, (s0, n) in enumerate(GR):
                        def fk(hh=hh, g=g, s0=s0, n=n):
                            b = nxt("pS", SBANKS) if lazy else nxt("pO", [2, 3, 4, 5])
                            for k in range(2):
                                MM(ps[b][0:64, :n], wkv2[:, k, hh * 128:hh * 128 + 64], ckvn3[:, k, s0:s0 + n], k == 0, k == 1,
                                   r=[wkv2k, ("ckvn", k, g)], w=[psk(b)])
                            COPY("dve", Ksets[st_][hh][0:64, s0:s0 + n], ps[b][0:64, :n], w=[psk(b), ("Kf", st_, hh, g)])
                        if lazy:
                            out.append(fk)
                        else:
                            fk()
                return out

            def vproj_B(W):
                wkv2, wkv2k = W["kv"]
                wv2 = wkv2.rearrange("p k (h c) -> p k h c", h=2)
                for tt, (s0, n) in enumerate(TT):
                    if tt % 4 == 0:
                        vb = nxt("pS", SBANKS)
                    c0 = (tt % 4) * 128
                    g = min(tt // 4, 4)
                    for k in range(2):
                        MM(ps[vb][:n, c0:c0 + 128].rearrange("p (h c) -> p h c", h=2), ckvn3[:, k, s0:s0 + n],
                           wv2[:, k, :, 64:128], k == 0, k == 1, r=[wkv2k, ("ckvn", k, g)], w=[psk(vb)])
                    pv3 = ps[vb][:, :].rearrange("p (t c) -> p t c", c=128)
                    if tt % 4 == 3:
                        COPY("dve", VB[0][:, tt - 3:tt + 1, 0:64], pv3[:, :, 0:64], w=[psk(vb), ("Vf", 0)])
                        COPY("dve", VB[1][:, tt - 3:tt + 1, 64:128], pv3[:, :, 64:128], w=[psk(vb), ("Vf", 1)])
                    elif tt == 16:
                        COPY("act", VB[0][0:16, 16, 0:64], ps[vb][0:16, 0:64], w=[psk(vb), ("Vf", 0)])
                        COPY("act", VB[1][0:16, 16, 64:128], ps[vb][0:16, 64:128], w=[psk(vb), ("Vf", 1)])

            P.mark("B0.proj")
            Wcur = load_proj_B(0)
            proj_B(0, Wcur, False)
            obanks[0] = [2, 3]
            SBANKS.extend([4, 5])
            ptl.extend([sq[0], sq[1]])
            chunk_cfg[0] = 3
            prev_fill = []
            carryB = []
            for u in range(8):
                slot = u % 2
                P.mark(f"B{u}.vproj")
                vproj_B(Wcur)
                next_proj = []
                if u + 1 < 8:
                    Wcur = load_proj_B(u + 1)
                    next_proj = proj_B(u + 1, Wcur, True)
                wo, wok = LOADW(lambda b: b[:, :], b_w_o[u * 128:(u + 1) * 128, :])
                this_fill = wo_tiles(wo, wok, slot, (lambda g, slot=slot: [("oTh", slot, g, 0), ("oTh", slot, g, 1)]))
                prev_fill = interleave(prev_fill, next_proj)

                def evac_map(hh, g, s0, n, ob, sbk, slot=slot):
                    r0 = hh * 64
                    q0 = 64 - r0
                    tr = nxt("tmpe", [4, 5])
                    RECIP(tmp[tr][r0:r0 + 64, :n], ps[ob][q0:q0 + 64, :n], w=[psk(ob), ("tmp", tr)])
                    TTo("dve", oTb[slot][r0:r0 + 64, s0:s0 + n], ps[ob][r0:r0 + 64, :n], tmp[tr][r0:r0 + 64, :n],
                        ALU.mult, r=[("tmp", tr)], w=[psk(ob), ("oTh", slot, g, hh)])

                P.mark(f"B{u}.attn")
                maps = []
                for hh in range(2):
                    kk = [("Kf", slot, hh, g) for g in range(5)]
                    kk += [("krope", hh, g) for g in range(5)] if slot == 0 else [("krope2", hh)]
                    maps.append((Ksets[slot][hh], Qsets[slot][hh], 0, 96, kk,
                                 (lambda g, hh=hh, slot=slot: ("Qf", slot, hh, g)), VB[hh], ("Vf", hh)))
                carryB = attention_unit(maps, 96 ** -0.5, evac_map, None, False, fillers=prev_fill, carry=carryB, last=(u == 7))
                prev_fill = this_fill
            P.mark("B.wo_last")
            for f in prev_fill:
                f()

        def ffn(l):
            VW = VT2 if l == 0 else VT3
            vwk = "VT2" if l == 0 else "VT3"
            cbcol = VG["cb0"] if l == 0 else VG["cb1"]
            def load_tile(j):
                return (LOADW(kview(8, 128), ksrc(w_in[l, :, j * 128:(j + 1) * 128])),
                        LOADW(kview(8, 128), ksrc(w_in[l, :, DFF + j * 128:DFF + (j + 1) * 128])))

            pre_ahead = [load_tile(FBLOCKS[0][0]), load_tile(FBLOCKS[0][1])]
            norm_phase(VG["ffn0"] if l == 0 else VG["ffn1"], order=(4, 0, 1, 2, 3))
            for i in range(2):
                MEMSET("pool", AR[i][:, 0:2], 0.0, w=[("ARz", i)])
            order = [4, 0, 1, 2, 3]
            pend_gate = [None]
            for bi, blk in enumerate(FBLOCKS):
                P.mark(f"F{l}.in{bi}")
                nj = len(blk)
                if bi == 0:
                    ahead = pre_ahead
                else:
                    ahead = [load_tile(blk[0])] + ([load_tile(blk[1])] if nj > 1 else [])
                for jj, j in enumerate(blk):
                    (wa, wak), (wb, wbk) = ahead.pop(0)
                    if jj + 2 < nj:
                        ahead.append(load_tile(blk[jj + 2]))
                    ai = j % 2
                    A = AR[ai]
                    w0 = VW[:, 0 * 22 + j:0 * 22 + j + 1]
                    w1 = VW[:, 1 * 22 + j:1 * 22 + j + 1]
                    w2 = VW[:, 2 * 22 + j:2 * 22 + j + 1]
                    cbs = VT1[:, cbcol + j:cbcol + j + 1]
                    prevkey = ("ARz", ai)
                    for g in order:
                        s0, n = GR[g]
                        c0 = (0 if g == 4 else 16 + 512 * g) + 2
                        ba, bb = nxt("pF4", [(0, 1), (2, 3), (4, 5), (6, 7)])
                        for k in range(8):
                            MM(ps[ba][:, :n], wa[:, k, :], xn3[:, k, s0:s0 + n], k == 0, k == 7,
                               r=[wak, ("xn", k, g)] + ([wbk] if k == 0 else []), w=[psk(ba)] + ([psk(bb)] if k == 0 else []))
                        for k in range(8):
                            MM(ps[bb][:, :n], wb[:, k, :], xn3[:, k, s0:s0 + n], k == 0, k == 7,
                               r=[wbk, ("xn", k, g)], w=[psk(bb)])
                        akey = ("AR", ai, g)
                        COPY("act", A[:, c0:c0 + n], ps[ba][:, :n], w=[psk(ba), akey])
                        tu = nxt("tmp", T6)
                        ACT(tmp[tu][:, :n], ps[ba][:, :n], AF.Identity, r=[vwk, "VT1"], w=[psk(ba), ("tmp", tu)],
                            scale=w2, bias=cbs)
                        STT("dve", tmp[tu][:, :n], A[:, c0 - 1:c0 - 1 + n], w1, tmp[tu][:, :n], ALU.mult, ALU.add,
                            r=[akey, prevkey, vwk], w=[("tmp", tu)])
                        STT("dve", tmp[tu][:, :n], A[:, c0 - 2:c0 - 2 + n], w0, tmp[tu][:, :n], ALU.mult, ALU.add,
                            r=[akey, prevkey, vwk], w=[("tmp", tu)])
                        tsl = nxt("tmp", T6)
                        if pend_gate[0] is not None:
                            pend_gate[0]()

                        def gate(jj=jj, s0=s0, n=n, bb=bb, tu=tu, tsl=tsl, g=g):
                            ACT(tmp[tsl][:, :n], tmp[tu][:, :n], AF.Silu, r=[("tmp", tu)], w=[("tmp", tsl)])
                            TTo("dve", gated3[:, jj, s0:s0 + n], ps[bb][:, :n], tmp[tsl][:, :n], ALU.mult,
                                r=[("tmp", tsl)], w=[psk(bb), ("gated", jj, g)])
                        pend_gate[0] = gate
                        prevkey = akey
                if pend_gate[0] is not None:
                    pend_gate[0]()
                    pend_gate[0] = None
                j0 = blk[0]
                P.mark(f"F{l}.out{bi}")
                for m in range(8):
                    wo, wok = LOADW(kview(nj, 128), w_out[l, j0 * 128:(j0 + nj) * 128, m * 128:(m + 1) * 128]
                                    .rearrange("(j p) c -> p j c", p=128))
                    for g, (s0, n) in enumerate(GR):
                        b = nxt("pW4", [4, 5, 6, 7])
                        for jj in range(nj):
                            MM(ps[b][:, :n], wo[:, jj, :], gated3[:, jj, s0:s0 + n], jj == 0, jj == nj - 1,
                               r=[wok, ("gated", jj, g)], w=[psk(b)])
                        TTo("dve", hT3[:, m, s0:s0 + n], ps[b][:, :n], hT3[:, m, s0:s0 + n], ALU.add,
                            w=[psk(b), ("hT", m, g)])

        def final():
            P.mark("final")
            ost = [arF[:, i * 1024:(i + 1) * 1024] for i in range(4)]
            for tt in range(16):
                s0 = tt * 128
                g = tt // 4
                b0, b1 = nxt("pF4", [(0, 1), (2, 3), (4, 5), (6, 7)])
                for k in range(8):
                    bk = b0 if k < 4 else b1
                    P.op("pe", lambda e, k=k, bk=bk, s0=s0: e.transpose(ps[bk][:, (k % 4) * 128:(k % 4 + 1) * 128],
                                                                         hT3[:, k, s0:s0 + 128], ident[:, :]),
                         r=[("hT", k, g), "ident"], w=[psk(bk)])
                ssc = nxt("ss", [0, 1, 2, 3]) * 4
                for hb, bk in enumerate((b0, b1)):
                    tq = nxt("tmp", T6)
                    ACT(tmp[tq][:, :], ps[bk][:, :], AF.Square, w=[psk(bk), ("tmp", tq)])
                    P.op("dve", lambda e, tq=tq, c=ssc + hb: e.reduce_sum(out=lamb[:, c:c + 1], in_=tmp[tq][:, :], axis=AX.X),
                         r=[("tmp", tq)], w=[("ss", ssc + hb)])
                TTo("dve", lamb[:, ssc + 2:ssc + 3], lamb[:, ssc:ssc + 1], lamb[:, ssc + 1:ssc + 2],
                    ALU.add, r=[("ss", ssc), ("ss", ssc + 1)], w=[("ss", ssc + 2)])
                ACT(lamb[:, ssc + 3:ssc + 4], lamb[:, ssc + 2:ssc + 3], AF.Sqrt, r=[("ss", ssc + 2), "cst"],
                    w=[("ss", ssc + 3)], scale=1.0 / D, bias=epsc)
                RECIP(lamb[:, ssc + 3:ssc + 4], lamb[:, ssc + 3:ssc + 4], w=[("ss", ssc + 3)])
                oi = tt % 4
                for hb, bk in enumerate((b0, b1)):
                    STT("dve", ost[oi][:, hb * 512:(hb + 1) * 512], ps[bk][:, :], lamb[:, ssc + 3:ssc + 4],
                        gfin[:, hb * 512:(hb + 1) * 512], ALU.mult, ALU.mult, r=[("ss", ssc + 3), "gfin"],
                        w=[psk(bk), ("ost", oi)])
                DMA("sp", out[s0:s0 + 128, :], ost[oi], gout[oi], r=[("ost", oi)], w=[("out", tt)])
            P.op("sp", lambda e: e.nop(), r=[("out", tt) for tt in range(16)])

        if do0:
            layer_A()
            P.barrier()
            ffn(0)
            P.barrier()
        if mode == "L0":
            DMA("sp", hmid[:, :], hT[:, :], gmisc, r=ALLHT, w=["hmid"])
            P.op("sp", lambda e: e.nop(), r=["hmid"])
        if do1:
            layer_B()
            P.barrier()
            ffn(1)
            P.barrier()
            final()
        P.mark("end")
        if PRINT_MARKS:
            print("MARKS", P.marks)
        P.finalize(nc, st)
        P.run_block(nc)
    return nc


def _rot_tables():
    pos = np.concatenate([np.arange(NMETA, NMETA + SEQ), np.arange(NMETA)]).astype(np.float32)
    f32 = np.float32

    def inv(theta, half):
        return (f32(1.0) / np.power(f32(theta), np.arange(half, dtype=np.float32) / f32(half))).astype(np.float32)

    tabA = np.zeros((2, 128, T), np.float32); tabA[0] = 1.0
    ia = inv(500000.0, 8)
    for r in range(128):
        d = r % 64
        if d < 16:
            ang = (pos * ia[d % 8]).astype(np.float32)
            tabA[0, r] = np.cos(ang)
            tabA[1, r] = -np.sin(ang) if d < 8 else np.sin(ang)
    tabB = np.zeros((2, 128, T), np.float32); tabB[0] = 1.0
    ib = inv(10000.0, 16)
    for r in range(64, 96):
        j = r - 64
        ang = (pos * ib[j % 16]).astype(np.float32)
        tabB[0, r] = np.cos(ang)
        tabB[1, r] = -np.sin(ang) if j < 16 else np.sin(ang)
    return tabA, tabB


def _prep(inputs):
    f = lambda a: np.ascontiguousarray(np.asarray(a, dtype=np.float32))
    I = {k: f(v) for k, v in inputs.items()}
    tabA, tabB = _rot_tables()
    w_qkv = I["a_w_qkv"][0]
    idx = np.arange(2048)
    d = idx % 64
    idx_sw = np.where(d < 8, idx + 8, np.where(d < 16, idx - 8, idx))
    w_qk_sw = np.ascontiguousarray(w_qkv[:, idx_sw])
    w_uq = I["b_w_uq"][0]
    idx = np.arange(1536)
    j = idx % 96
    idx_sw = np.where((j >= 64) & (j < 80), idx + 16, np.where(j >= 80, idx - 16, idx))
    w_uq_sw = np.ascontiguousarray(w_uq[:, idx_sw])
    w_dkv = I["kv_w_dkv"]
    w_dkv_r = np.zeros((D, 96), np.float32); w_dkv_r[:, 64:96] = w_dkv[:, 256:288]
    w_dkv_rs = np.zeros((D, 96), np.float32); w_dkv_rs[:, 64:96] = w_dkv[:, 256 + (np.arange(32) + 16) % 32]
    vecs1 = np.concatenate([
        I["a_attn_norm"][0].reshape(8, 128), I["ffn_norm"][0].reshape(8, 128), I["kv_norm"].reshape(8, 128),
        I["b_attn_norm"][0].reshape(8, 128), I["ffn_norm"][1].reshape(8, 128), I["kv_a_norm"].reshape(2, 128),
        I["b_q_a_norm"][0].reshape(3, 128), I["a_sub_norm"][0].reshape(1, 128),
        I["ffn_conv_b"][0].reshape(22, 128), I["ffn_conv_b"][1].reshape(22, 128)], axis=0)
    vecs2 = I["ffn_conv_w"][0].reshape(66, 128)
    vecs3 = I["ffn_conv_w"][1].reshape(66, 128)
    lam4 = np.stack([I["a_lambda_q1"][0], I["a_lambda_k1"][0], I["a_lambda_q2"][0], I["a_lambda_k2"][0]], axis=0)
    shared0 = dict(meta=I["meta_tokens"], w_qkv=w_qkv, w_qk_sw=w_qk_sw, a_w_o=I["a_w_o"][0], lam4=f(lam4), tabA=tabA)
    shared1 = dict(w_dkv=w_dkv, w_dkv_r=w_dkv_r, w_dkv_rs=w_dkv_rs, w_ukv=I["kv_w_ukv"], w_dq=I["b_w_dq"][0],
                   w_uq=w_uq, w_uq_sw=w_uq_sw, b_w_o=I["b_w_o"][0], tabB=tabB, final_norm=I["final_norm"])
    sharedf = dict(w_in=I["ffn_w_in"], w_out=I["ffn_w_out"], vecs1=f(vecs1), vecs2=f(vecs2), vecs3=f(vecs3))
    return I, shared0, shared1, sharedf


FUSED = True
_NC_CACHE = {}


def _get_nc(mode):
    if mode not in _NC_CACHE:
        _NC_CACHE[mode] = build_nc(mode)
    return _NC_CACHE[mode]


def kernel(**inputs):
    I, s0, s1, sf = _prep(inputs)
    B = I["x"].shape[0]
    cores = list(range(B))
    if FUSED:
        nc = _get_nc("ALL")
        maps = [dict(x=I["x"][b], **s0, **s1, **sf) for b in range(B)]
        res = run_bass_kernel_spmd(nc, maps, core_ids=cores)
        return np.stack([np.asarray(r["out"], dtype=np.float32) for r in res.results], axis=0)
    nc0 = _get_nc("L0")
    maps = [dict(x=I["x"][b], **s0, **sf) for b in range(B)]
    res0 = run_bass_kernel_spmd(nc0, maps, core_ids=cores)
    nc1 = _get_nc("L1")
    maps = [dict(hmid=np.asarray(res0.results[b]["hmid"], dtype=np.float32), **s1, **sf) for b in range(B)]
    res1 = run_bass_kernel_spmd(nc1, maps, core_ids=cores)
    return np.stack([np.asarray(r["out"], dtype=np.float32) for r in res1.results], axis=0)
```
